# Optimizing a Trainium2 kernel written in Bass

```python
import math
import jax, jax.numpy as jnp
from jax import lax
import numpy as np

D_MODEL = 1024
BATCH = 16
SEQ = 2048
DEPTH = 1
DEC_BATCH = 128
DEC_SEQ = 8
PAST_LEN = 16384
PAGE_SIZE = 128

N_HEADS = 16
N_KV_HEADS = 4
HEAD_DIM = 64
KV_GROUP = N_HEADS // N_KV_HEADS
ATTN_WIDTH = N_HEADS * HEAD_DIM
KV_WIDTH = N_KV_HEADS * HEAD_DIM
WINDOW = 128
W_BUF = min(WINDOW, PAST_LEN)
ROPE_DIM = HEAD_DIM // 4
ROPE_THETA = 500000.0
SSD_WIDTH = 2 * D_MODEL
SSD_HEAD_DIM = 64
SSD_HEADS = SSD_WIDTH // SSD_HEAD_DIM
SSD_GROUPS = 4
SSD_HPG = SSD_HEADS // SSD_GROUPS
D_STATE = 128
CONV_WIDTH = 4
CONV_DIM = SSD_WIDTH + 2 * SSD_GROUPS * D_STATE
SSD_CHUNK = 128
N_BRANCH = 2
EPS = 1e-6
OFF_Q = ATTN_WIDTH
OFF_K = OFF_Q + KV_WIDTH
OFF_V = OFF_K + KV_WIDTH
OFF_ZA = OFF_V + ATTN_WIDTH
OFF_ZS = OFF_ZA + SSD_WIDTH
OFF_XBC = OFF_ZS + CONV_DIM
OFF_DT = OFF_XBC + SSD_HEADS
IN_DIM = OFF_DT + N_BRANCH * D_MODEL

kernel_name = "hybrid_swa_sink_ssd_gated_step"


def rms_norm(x, w):
    xf = x.astype(jnp.float32)
    y = xf * lax.rsqrt(jnp.mean(xf * xf, -1, keepdims=True) + EPS)
    return (y * w.astype(jnp.float32)).astype(x.dtype)


def partial_rope(x, pos):
    half = ROPE_DIM // 2
    inv_freq = jnp.power(ROPE_THETA, -jnp.arange(half, dtype=jnp.float32) * (2.0 / ROPE_DIM))
    ang = pos.astype(jnp.float32)[:, None] * inv_freq[None, :]
    cos = jnp.cos(ang)[None, :, None, :]
    sin = jnp.sin(ang)[None, :, None, :]
    xf = x.astype(jnp.float32)
    x1, x2 = xf[..., :half], xf[..., half:ROPE_DIM]
    out = jnp.concatenate([x1 * cos - x2 * sin, x2 * cos + x1 * sin, xf[..., ROPE_DIM:]], -1)
    return out.astype(x.dtype)


def sink_attention(q, k, v, mask, sinks):
    s = jnp.einsum('bnqkgd,bnskd->bnkgqs', q, k).astype(jnp.float32) * (HEAD_DIM ** -0.5)
    s = jnp.where(mask[None, :, None, None], s, -jnp.inf)
    sk = sinks.astype(jnp.float32).reshape(N_KV_HEADS, KV_GROUP)[:, :, None, None]
    m = jnp.maximum(s.max(-1, keepdims=True), sk)
    p = jnp.exp(s - m)
    p = p / (p.sum(-1, keepdims=True) + jnp.exp(sk - m))
    return jnp.einsum('bnkgqs,bnskd->bnqkgd', p.astype(v.dtype), v)


def window_attention_prompt(q, k, v, sinks):
    b, L = q.shape[:2]
    nb = L // WINDOW
    qb = q.reshape(b, nb, WINDOW, N_KV_HEADS, KV_GROUP, HEAD_DIM)

    def band(t):
        tb = t.reshape(b, nb, WINDOW, N_KV_HEADS, HEAD_DIM)
        prev = jnp.pad(tb, ((0, 0), (1, 0), (0, 0), (0, 0), (0, 0)))[:, :-1]
        return jnp.concatenate([prev, tb], axis=2)

    blk = jnp.arange(nb)[:, None] * WINDOW
    qpos = blk + jnp.arange(WINDOW)[None]
    kpos = blk - WINDOW + jnp.arange(2 * WINDOW)[None]
    d = qpos[:, :, None] - kpos[:, None, :]
    mask = (d >= 0) & (d < WINDOW) & (kpos[:, None, :] >= 0)
    o = sink_attention(qb, band(k), band(v), mask, sinks)
    return o.reshape(b, L, ATTN_WIDTH)


def window_attention_decode(q, k_new, v_new, cache_k, cache_v, sinks):
    b, T = q.shape[:2]
    nbuf = cache_k.shape[1]
    k_all = jnp.concatenate([cache_k.astype(k_new.dtype), k_new], 1)
    v_all = jnp.concatenate([cache_v.astype(v_new.dtype), v_new], 1)
    qpos = PAST_LEN + jnp.arange(T)
    kpos = PAST_LEN - nbuf + jnp.arange(nbuf + T)
    d = qpos[:, None] - kpos[None, :]
    mask = (d >= 0) & (d < WINDOW)
    o = sink_attention(q.reshape(b, 1, T, N_KV_HEADS, KV_GROUP, HEAD_DIM),
                       k_all[:, None], v_all[:, None], mask[None], sinks)
    return o.reshape(b, T, ATTN_WIDTH), k_all[:, -nbuf:], v_all[:, -nbuf:]


def causal_conv(xbc, conv_state, w, bias):
    xp = jnp.concatenate([conv_state.astype(xbc.dtype), xbc], 1)
    y = lax.conv_general_dilated(xp, w[:, None, :].astype(xp.dtype), (1,), 'VALID',
                                 dimension_numbers=('NWC', 'WIO', 'NWC'),
                                 feature_group_count=CONV_DIM)
    return jax.nn.silu(y + bias.astype(y.dtype)), xp[:, -(CONV_WIDTH - 1):]


def ssd_scan(x, dt, A, Bm, Cm, h0, chunk):
    b, L = x.shape[:2]
    c = L // chunk
    xs = x.reshape(b, c, chunk, SSD_GROUPS, SSD_HPG, SSD_HEAD_DIM)
    dts = dt.reshape(b, c, chunk, SSD_GROUPS, SSD_HPG)
    Bs = Bm.reshape(b, c, chunk, SSD_GROUPS, D_STATE)
    Cs = Cm.reshape(b, c, chunk, SSD_GROUPS, D_STATE)
    a_cs = jnp.cumsum(dts * A.reshape(SSD_GROUPS, SSD_HPG), axis=2)
    seg = a_cs[:, :, :, None] - a_cs[:, :, None]
    causal = jnp.tril(jnp.ones((chunk, chunk), dtype=bool))
    Lm = jnp.exp(jnp.where(causal[:, :, None, None], seg, -jnp.inf))
    cb = jnp.einsum('bclgn,bcsgn->bclsg', Cs, Bs)
    dx = dts[..., None] * xs
    y_diag = jnp.einsum('bclsg,bclsgh,bcsghp->bclghp', cb, Lm, dx)
    decay_to_end = jnp.exp(a_cs[:, :, -1:] - a_cs)
    states = jnp.einsum('bclgn,bclgh,bclghp->bcghpn', Bs, decay_to_end * dts, xs)
    chunk_decay = jnp.exp(a_cs[:, :, -1])

    def step(h, inp):
        st, dec = inp
        return h * dec[..., None, None] + st, h

    h0g = h0.reshape(b, SSD_GROUPS, SSD_HPG, SSD_HEAD_DIM, D_STATE)
    h_last, h_prev = lax.scan(step, h0g, (jnp.moveaxis(states, 1, 0), jnp.moveaxis(chunk_decay, 1, 0)))
    h_prev = jnp.moveaxis(h_prev, 0, 1)
    y_off = jnp.einsum('bclgn,bcghpn,bclgh->bclghp', Cs, h_prev, jnp.exp(a_cs))
    y = (y_diag + y_off).reshape(b, L, SSD_HEADS, SSD_HEAD_DIM)
    return y, h_last.reshape(b, SSD_HEADS, SSD_HEAD_DIM, D_STATE)


def ssd_branch(xbc_raw, z_s, dt_raw, conv_state, ssm_state, p, chunk):
    b, L = xbc_raw.shape[:2]
    xbc, new_conv = causal_conv(xbc_raw, conv_state, p['conv_w'], p['conv_b'])
    xs, Bm, Cm = jnp.split(xbc.astype(jnp.float32), [SSD_WIDTH, SSD_WIDTH + SSD_GROUPS * D_STATE], axis=-1)
    dt = jax.nn.softplus(dt_raw.astype(jnp.float32) + p['dt_bias'].astype(jnp.float32))
    A = -jnp.exp(p['A_log'].astype(jnp.float32))
    xh = xs.reshape(b, L, SSD_HEADS, SSD_HEAD_DIM)
    y, h_last = ssd_scan(xh, dt, A, Bm.reshape(b, L, SSD_GROUPS, D_STATE),
                         Cm.reshape(b, L, SSD_GROUPS, D_STATE), ssm_state.astype(jnp.float32), chunk)
    y = y + p['D_skip'].astype(jnp.float32)[:, None] * xh
    y = y.reshape(b, L, SSD_WIDTH) * jax.nn.silu(z_s.astype(jnp.float32))
    yg = y.reshape(b, L, SSD_GROUPS, SSD_WIDTH // SSD_GROUPS)
    yg = yg * lax.rsqrt(jnp.mean(yg * yg, -1, keepdims=True) + EPS)
    y = yg.reshape(b, L, SSD_WIDTH) * p['ssd_norm_w'].astype(jnp.float32)
    return y.astype(xbc_raw.dtype), new_conv, h_last.astype(ssm_state.dtype)


def layer_forward(x, pos, cache_k, cache_v, conv_state, ssm_state, p, decode):
    b, L, _ = x.shape
    h = rms_norm(x, p['norm_w'])
    proj = h @ p['w_in'].astype(h.dtype)
    q, k, v, z_a, z_s, xbc, dt_raw, g = jnp.split(
        proj, [OFF_Q, OFF_K, OFF_V, OFF_ZA, OFF_ZS, OFF_XBC, OFF_DT], axis=-1)
    q = partial_rope(rms_norm(q.reshape(b, L, N_HEADS, HEAD_DIM), p['q_norm_w']), pos)
    k = partial_rope(rms_norm(k.reshape(b, L, N_KV_HEADS, HEAD_DIM), p['k_norm_w']), pos)
    v = v.reshape(b, L, N_KV_HEADS, HEAD_DIM)
    if decode:
        o_a, new_k, new_v = window_attention_decode(q, k, v, cache_k, cache_v, p['sinks'])
        chunk = L
    else:
        o_a = window_attention_prompt(q, k, v, p['sinks'])
        nkeep = min(WINDOW, L)
        new_k, new_v = k[:, -nkeep:], v[:, -nkeep:]
        chunk = SSD_CHUNK
    o_s, new_conv, new_ssm = ssd_branch(xbc, z_s, dt_raw, conv_state, ssm_state, p, chunk)
    p_a = (o_a * jax.nn.silu(z_a)) @ p['w_attn_proj'].astype(o_a.dtype)
    p_s = o_s @ p['w_ssd_proj'].astype(o_s.dtype)
    g_a, g_s = jnp.split(g, 2, axis=-1)
    merged = jax.nn.sigmoid(g_a) * p_a + jax.nn.sigmoid(g_s) * p_s
    y = x + merged @ p['w_out'].astype(merged.dtype)
    return y, new_k, new_v, new_conv, new_ssm


def setup_inputs(seed: int = 0) -> dict:
    key = jax.random.key(seed)
    ks = jax.random.split(key, 24)
    f32 = jnp.float32
    nrm = lambda k, s, sc: jax.random.normal(k, s, f32) * sc
    dt0 = jnp.exp(jax.random.uniform(ks[0], (DEPTH, SSD_HEADS), f32, math.log(1e-3), math.log(1e-1)))
    return {
        'x_prompt': nrm(ks[1], (BATCH, SEQ, D_MODEL), 1.0),
        'x_sample': nrm(ks[2], (DEC_BATCH, DEC_SEQ, D_MODEL), 1.0),
        'cache_k': nrm(ks[3], (DEPTH, DEC_BATCH, W_BUF, N_KV_HEADS, HEAD_DIM), 1.0),
        'cache_v': nrm(ks[4], (DEPTH, DEC_BATCH, W_BUF, N_KV_HEADS, HEAD_DIM), 1.0),
        'state_conv': nrm(ks[5], (DEPTH, DEC_BATCH, CONV_WIDTH - 1, CONV_DIM), 1.0),
        'state_ssm': nrm(ks[6], (DEPTH, DEC_BATCH, SSD_HEADS, SSD_HEAD_DIM, D_STATE), 0.1),
        'norm_w': 1.0 + nrm(ks[7], (DEPTH, D_MODEL), 0.02),
        'w_in': nrm(ks[8], (DEPTH, D_MODEL, IN_DIM), D_MODEL ** -0.5),
        'q_norm_w': 1.0 + nrm(ks[9], (DEPTH, HEAD_DIM), 0.02),
        'k_norm_w': 1.0 + nrm(ks[10], (DEPTH, HEAD_DIM), 0.02),
        'sinks': nrm(ks[11], (DEPTH, N_HEADS), 1.0),
        'conv_w': nrm(ks[12], (DEPTH, CONV_WIDTH, CONV_DIM), CONV_WIDTH ** -0.5),
        'conv_b': nrm(ks[13], (DEPTH, CONV_DIM), 0.01),
        'dt_bias': dt0 + jnp.log(-jnp.expm1(-dt0)),
        'A_log': jnp.log(jax.random.uniform(ks[14], (DEPTH, SSD_HEADS), f32, 1.0, 16.0)),
        'D_skip': 1.0 + nrm(ks[15], (DEPTH, SSD_HEADS), 0.02),
        'ssd_norm_w': 1.0 + nrm(ks[16], (DEPTH, SSD_WIDTH), 0.02),
        'w_attn_proj': nrm(ks[17], (DEPTH, ATTN_WIDTH, D_MODEL), ATTN_WIDTH ** -0.5),
        'w_ssd_proj': nrm(ks[18], (DEPTH, SSD_WIDTH, D_MODEL), SSD_WIDTH ** -0.5),
        'w_out': nrm(ks[19], (DEPTH, D_MODEL, D_MODEL), D_MODEL ** -0.5),
    }


def reference(x_prompt, x_sample, cache_k, cache_v, state_conv, state_ssm,
              norm_w, w_in, q_norm_w, k_norm_w, sinks, conv_w, conv_b, dt_bias,
              A_log, D_skip, ssd_norm_w, w_attn_proj, w_ssd_proj, w_out):
    pos_p = jnp.arange(x_prompt.shape[1], dtype=jnp.int32)
    pos_s = PAST_LEN + jnp.arange(x_sample.shape[1], dtype=jnp.int32)
    bp = x_prompt.shape[0]
    yp, ys = x_prompt, x_sample
    pk, pv, pc, pss, sk_, sv, sc, sss = [], [], [], [], [], [], [], []
    for l in range(DEPTH):
        p = dict(norm_w=norm_w[l], w_in=w_in[l], q_norm_w=q_norm_w[l], k_norm_w=k_norm_w[l],
                 sinks=sinks[l], conv_w=conv_w[l], conv_b=conv_b[l], dt_bias=dt_bias[l],
                 A_log=A_log[l], D_skip=D_skip[l], ssd_norm_w=ssd_norm_w[l],
                 w_attn_proj=w_attn_proj[l], w_ssd_proj=w_ssd_proj[l], w_out=w_out[l])
        zc = jnp.zeros((bp, CONV_WIDTH - 1, CONV_DIM), x_prompt.dtype)
        zs = jnp.zeros((bp, SSD_HEADS, SSD_HEAD_DIM, D_STATE), state_ssm.dtype)
        yp, k1, v1, c1, s1 = layer_forward(yp, pos_p, None, None, zc, zs, p, False)
        ys, k2, v2, c2, s2 = layer_forward(ys, pos_s, cache_k[l], cache_v[l], state_conv[l], state_ssm[l], p, True)
        pk.append(k1); pv.append(v1); pc.append(c1); pss.append(s1)
        sk_.append(k2); sv.append(v2); sc.append(c2); sss.append(s2)
    return (yp, ys, jnp.stack(pk), jnp.stack(pv), jnp.stack(pc), jnp.stack(pss),
            jnp.stack(sk_), jnp.stack(sv), jnp.stack(sc), jnp.stack(sss))
```

```python
import contextlib
import math
import os
STAGE = int(os.environ.get('KSTAGE', '99'))
SUB = int(os.environ.get('KSUB', '99'))
KQ = int(os.environ.get('KQ', '99'))
import numpy as np
import concourse.bass as bass
import concourse.mybir as mybir
from concourse.bass_utils import run_bass_kernel_spmd

F32 = mybir.dt.float32
BF16 = mybir.dt.bfloat16
ALU = mybir.AluOpType
AF = mybir.ActivationFunctionType

NCORES = 8
D = 1024
SEQ = 2048
NSEQ = 2
NB = 16
TS = 8
PAST = 16384
EPS = 1e-6
NT = 256
NG = 27
NDS = 24

G_Q, G_K, G_ZA, G_ZS, G_XBC, G_GA, G_GS, G_PA, G_PS, G_WO = 0, 2, 3, 5, 9, 15, 17, 19, 21, 25

OFF_Q, OFF_K, OFF_V, OFF_ZA, OFF_ZS, OFF_XBC, OFF_DT, OFF_G = 0, 1024, 1280, 1536, 2560, 4608, 7680, 7712

P_NW, P_WQ, P_WK, P_CW, P_CB, P_DTB, P_ALOG, P_D, P_NWS, P_SINK = 0, 8, 9, 10, 106, 130, 162, 194, 210, 226
NPRM = 242
C_ID, C_ONES, C_ONESE, C_ONESO, C_BLK, C_RT, C_M4, C_U, C_L, C_UB, C_LB, C_SEL, C_MSC, C_MSN = (
    0, 128, 256, 384, 512, 640, 768, 1280, 1408, 1536, 1664, 1792, 1808, 1816)
C_E0 = 1816 + 128
NCB = C_E0 + 128
F_ID, F_COSS, F_SINS, F_SEL = 0, 128, 256, 384
NCF = 400


class KB:
    def __init__(self, nc, es):
        self.nc = nc
        self.eng = {'pe': nc.tensor, 'act': nc.scalar, 'dve': nc.vector, 'pool': nc.gpsimd, 'sp': nc.sync}
        self.semobj = {}
        for e in self.eng:
            self.semobj[e] = es.enter_context(nc.semaphore("s_" + e))
        for i in range(NDS):
            self.semobj["d%d" % i] = es.enter_context(nc.semaphore("sd%d" % i))
        self.cnt = {e: 0 for e in self.eng}
        self.known = {e: {} for e in self.eng}
        self.state = {}
        self.dcnt = [0] * NDS
        self.dq = {}
        self.pb = 0
        self.nwait = 0
        self.ninst = 0

    def _wait(self, e, k, v):
        if self.known[e].get(k, 0) >= v:
            return
        self.eng[e].wait_ge(self.semobj[k], v)
        self.known[e][k] = v
        self.nwait += 1

    def _need(self, e, reads, writes):
        need = {}

        def add(k, v):
            if need.get(k, 0) < v:
                need[k] = v
        for key in reads:
            st = self.state.get(key)
            if st and st[0]:
                add(*st[0])
            if st and isinstance(key, tuple) and key[0] == 'ps':
                for k, v in st[1].items():
                    if k != e:
                        add(k, v)
        for key in writes:
            st = self.state.get(key)
            if st:
                if st[0]:
                    add(*st[0])
                for k, v in st[1].items():
                    add(k, v)
        for k, v in need.items():
            if k == e and e == 'pe':
                continue
            self._wait(e, k, v)

    def _commit(self, ev, reads, writes):
        for key in writes:
            self.state[key] = [ev, {}]
        for key in reads:
            st = self.state.setdefault(key, [None, {}])
            if st[1].get(ev[0], 0) < ev[1]:
                st[1][ev[0]] = ev[1]

    def op(self, e, fn, reads=(), writes=(), inc=True):
        self._need(e, reads, writes)
        inst = fn(self.eng[e])
        self.ninst += 1
        if inc:
            self.cnt[e] += 1
            inst.then_inc(self.semobj[e], 1)
            ev = (e, self.cnt[e])
        else:
            ev = (e, self.cnt[e] + 1)
        self._commit(ev, reads, writes)

    def dma(self, q, out, in_, reads=(), writes=()):
        lo, hi = (0, 8) if q == 'pool' else (8, NDS)
        i = self.dq.get(q, lo)
        self.dq[q] = i + 1 if i + 1 < hi else lo
        k = "d%d" % i
        if self.dcnt[i]:
            self._wait(q, k, self.dcnt[i])
        self._need(q, reads, writes)
        self.dcnt[i] += 16
        self.eng[q].dma_start(out=out, in_=in_).then_inc(self.semobj[k], 16)
        self.ninst += 1
        self._commit((k, self.dcnt[i]), reads, writes)

    def psum(self):
        i = self.pb
        self.pb = (self.pb + 1) % 8
        return i

    def finish(self):
        for i in range(NDS):
            if self.dcnt[i]:
                self._wait('sp', "d%d" % i, self.dcnt[i])
        for e in ('pe', 'act', 'dve', 'pool'):
            if self.cnt[e]:
                self._wait('sp', e, self.cnt[e])


def build_program(do_sample=True, max_tiles=None):
    nc = bass.Bass("TRN2", target_bir_lowering=False)

    def din(name, shape):
        return nc.dram_tensor(name, list(shape), F32, kind="ExternalInput").ap()

    def dout(name, shape):
        return nc.dram_tensor(name, list(shape), F32, kind="ExternalOutput").ap()

    xp = din("xp", [NSEQ * SEQ, D])
    xs = din("xs", [NB * TS, D])
    wall = din("wall", [NG, 128, 8, 512])
    wA = din("wA", [128, 8, 288])
    prm_d = din("prm", [128, NPRM])
    cb_d = din("cb", [128, NCB])
    cf_d = din("cf", [128, NCF])
    rope_d = din("rope", [2, 128, SEQ])
    ck_d = din("ck", [NB, 128, 256])
    cv_d = din("cv", [NB, 128, 256])
    ckT_d = din("ckT", [NB, 256, 128])
    scT_d = din("sconvT", [3072, NB * 3])
    ss_d = din("sssm", [NB, 2048, 128])
    ssT_d = din("sssmT", [NB, 128, 2048])

    yp = dout("yp", [NSEQ * SEQ, D])
    ys = dout("ys", [NB * TS, D])
    kwp = dout("kwp", [NSEQ, 128, 256])
    vwp = dout("vwp", [NSEQ, 128, 256])
    cvp = dout("cvp", [NSEQ, 3, 3072])
    ssp = dout("ssp", [NSEQ, 2048, 128])
    kws = dout("kws", [NB, 128, 256])
    vws = dout("vws", [NB, 128, 256])
    cvs = dout("cvs", [NB, 3, 3072])
    sss = dout("sss", [NB, 2048, 128])
    scr = nc.dram_tensor("scr", [128, 3072], F32, kind="Internal").ap()
    wbf = nc.dram_tensor("wbf", [NG, 128, 8, 512], BF16, kind="Internal").ap()

    es = contextlib.ExitStack()
    with es:
        kb = KB(nc, es)

        def sb(name, shape, dt=BF16):
            return es.enter_context(nc.sbuf_tensor(name, list(shape), dt))

        banks = [es.enter_context(nc.psum_tensor("bank%d" % i, [128, 512], F32)) for i in range(8)]

        def bank():
            i = kb.psum()
            return banks[i], ('ps', i)

        wbuf = [sb("wbuf%d" % i, [128, 8, 512]) for i in range(3)]
        wAt = sb("wAt", [128, 8, 288])
        prm = sb("prm_sb", [128, NPRM], F32)
        cb = sb("cb_sb", [128, NCB])
        cf = sb("cf_sb", [128, NCF], F32)
        rope = sb("rope_sb", [128, 2, NT], F32)
        xin = [sb("xin%d" % i, [128, D], F32) for i in range(2)]
        xn = sb("xn", [128, D])
        hT = sb("hT", [128, 8, NT])
        qT = sb("qT", [128, 8, NT])
        mT = qT
        kTE = sb("kTE", [128, 4, 128 + NT])
        kTO = sb("kTO", [128, 4, 128 + NT])
        NCH = NT // 128
        VE = sb("VE", [128, 1 + NCH, 4, 128])
        VO = sb("VO", [128, 1 + NCH, 4, 128])
        zaT = sb("zaT", [128, 8, NT])
        ogT = zaT
        zsT = sb("zsT", [128, 16, NT])
        xbcT = sb("xbcT", [128, 24, NT])
        gT = sb("gT", [128, 16, NT])
        ynT = sb("ynT", [128, 16, NT])
        qraw = [sb("qraw%d" % i, [128, NT]) for i in range(2)]
        qsq = [sb("qsq%d" % i, [128, NT]) for i in range(2)]
        t1 = [sb("t1_%d" % i, [128, NT], F32) for i in range(2)]
        t2 = [sb("t2_%d" % i, [128, NT], F32) for i in range(2)]
        rsq = [sb("rsq%d" % i, [128, NT], F32) for i in range(2)]
        kf32 = sb("kf32", [128, 4, 128], F32)
        xraw = [sb("xraw%d" % i, [128, 3 + NT], F32) for i in range(4)]
        acc = [sb("acc%d" % i, [128, NT], F32) for i in range(4)]
        hist = sb("hist", [128, 24, 3], F32)
        Pt = [sb("Pt%d" % i, [128, 512]) for i in range(6)]
        Rr = [sb("Rr%d" % i, [128, 128], F32) for i in range(3)]
        ogf = [sb("ogf%d" % i, [128, 128], F32) for i in range(3)]
        sinkbc = sb("sinkbc", [128, 8, 128])
        esink = sb("esink", [128, 16], F32)
        RTq = sb("RTq", [128, 128])
        RTk = sb("RTk", [128, 128])
        Aneg = sb("Aneg", [128, 32], F32)
        sm = sb("sm", [128, 8], F32)
        dtt3 = sb("dtt3", [128, NT // 128, 32], F32)
        av3 = sb("av3", [128, NT // 128, 32], F32)
        ahi3 = sb("ahi3", [128, NT // 128, 32])
        alo3 = sb("alo3", [128, NT // 128, 32])
        atmp3 = sb("atmp3", [128, NT // 128, 32], F32)
        dte3 = sb("dte3", [128, NT // 128, 32], F32)
        cdb3 = sb("cdb3", [128, NT // 128, 32], F32)
        abcH = sb("abcH", [128, 32, 64])
        abcL = sb("abcL", [128, 32, 64])
        dxE = sb("dxE", [128, 16, 128])
        dxO = sb("dxO", [128, 16, 128])
        dxe = sb("dxe", [128, 2048])
        Btok = sb("Btok", [128, 512])
        AUh = [sb("AUh%d" % i, [128, 8, 128]) for i in range(2)]
        AUl = [sb("AUl%d" % i, [128, 8, 128]) for i in range(2)]
        Eb = [sb("Eb%d" % i, [128, 8, 128]) for i in range(2)]
        cbm = [sb("cbm%d" % i, [128, 128]) for i in range(2)]
        ea = [sb("ea%d" % i, [128, 128], F32) for i in range(4)]
        yt = [sb("yt%d" % i, [128, 128], F32) for i in range(4)]
        yb = sb("yb", [128, 16, 128])
        ysq = sb("ysq", [128, 16, 128])
        rsn = sb("rsn", [128, 4, 128], F32)
        S = sb("S", [128, 2048], F32)
        Sb = sb("Sb", [128, 2048])
        sttr = [sb("sttr%d" % i, [128, 512], F32) for i in range(3)]
        rr = {'se': 0, 'q': 0, 'x': 0, 'p': 0, 'e': 0, 'y': 0, 'yb': 0, 'm1': 0, 'st': 0, 'xin': 0}

        def rot(name, n=2):
            i = rr[name]
            rr[name] = (i + 1) % n
            return i

        print("sbuf bytes remaining/partition:", nc.sbuf_bytes_remaining)

        kb.dma('sp', prm[:], prm_d, writes=['prm'])
        kb.dma('sp', cf[:], cf_d, writes=['cf'])
        kb.dma('pool', cb[:], cb_d, writes=['cb'])
        kb.dma('pool', wAt[:], wA, writes=['wAt'])
        ident = cb[:, C_ID:C_ID + 128]
        ones = cb[:, C_ONES:C_ONES + 128]
        onesE = cb[:, C_ONESE:C_ONESE + 128]
        onesO = cb[:, C_ONESO:C_ONESO + 128]
        blk = cb[:, C_BLK:C_BLK + 128]
        identf = cf[:, F_ID:F_ID + 128]
        kb.op('dve', lambda e: e.tensor_scalar(out=RTq[:], in0=cb[:, C_RT:C_RT + 128], scalar1=prm[:, P_WQ:P_WQ + 1],
                                               scalar2=None, op0=ALU.mult), reads=['cb', 'prm'], writes=['RTq'])
        kb.op('dve', lambda e: e.tensor_scalar(out=RTk[:], in0=cb[:, C_RT:C_RT + 128], scalar1=prm[:, P_WK:P_WK + 1],
                                               scalar2=None, op0=ALU.mult), reads=['cb', 'prm'], writes=['RTk'])
        kb.op('act', lambda e: e.activation(out=Aneg[:], in_=prm[:, P_ALOG:P_ALOG + 32], func=AF.Exp),
              reads=['prm'], writes=['Aneg'])
        kb.op('dve', lambda e: e.tensor_scalar(out=Aneg[:], in0=Aneg[:], scalar1=-1.0, scalar2=None, op0=ALU.mult),
              reads=['Aneg'], writes=['Aneg'])
        kb.op('act', lambda e: e.activation(out=esink[:], in_=prm[:, P_SINK:P_SINK + 16], func=AF.Exp),
              reads=['prm'], writes=['esink'])
        kb.op('dve', lambda e: e.tensor_copy(
            out=sinkbc[:, :, :].rearrange("p i (e d) -> p i e d", e=2),
            in_=esink[:, :].rearrange("p (i e) -> p i e", e=2).unsqueeze(3).to_broadcast([128, 8, 2, 64])),
            reads=['esink'], writes=['sinkbc'])
        kb.op('pool', lambda e: e.memset(kTE[:], 0.0), writes=[('kT', g_) for g_ in range(4)] + ['kTprev'])
        kb.op('pool', lambda e: e.memset(kTO[:], 0.0), writes=[('kT', g_) for g_ in range(4)] + ['kTprev'])
        for t_, nm in ((VE, 'VE'), (VO, 'VO'), (dxE, 'dxE'), (dxO, 'dxO')):
            kb.op('pool', lambda e, t_=t_: e.memset(t_[:], 0.0), writes=[nm] if nm[0] == 'd' else [('V', s_) for s_ in range(NCH + 1)])
        kb.op('pool', lambda e: e.memset(S[:], 0.0), writes=['S'])
        kb.op('pool', lambda e: e.memset(Sb[:], 0.0), writes=['Sb'])
        kb.op('pool', lambda e: e.memset(hist[:], 0.0), writes=[('hist', ci) for ci in range(24)])

        wstate = {'issued': 0, 'used': 0}
        WORDER = list(range(5, 15)) + list(range(0, 5)) + list(range(15, NG))

        def load_w(g):
            k = wstate['used']
            assert WORDER[k % NG] == g, (k, g)
            wstate['used'] = k + 1
            while wstate['issued'] <= k + 2 and wstate['issued'] < wstate['total']:
                j = wstate['issued']
                gj = WORDER[j % NG]
                if j < NG:
                    kb.dma('pool', wbf[gj], wall[gj], writes=[('wbf', gj)])
                kb.dma('pool', wbuf[j % 3][:], wbf[gj], reads=[('wbf', gj)], writes=[('w', j % 3)])
                wstate['issued'] = j + 1
            return k % 3

        def rmsnorm_to_hT(xsrc_rows, ntok_chunks):
            for c in range(ntok_chunks):
                xi = rot('xin')
                kb.dma('sp', xin[xi][:], xsrc_rows[c * 128:(c + 1) * 128, :], writes=[('xin', xi)])
                kb.op('dve', lambda e: e.memset(sm[:, 0:1], 0.0), writes=['sm'])
                kb.op('act', lambda e: e.activation(out=xn[:], in_=xin[xi][:], func=AF.Square, accum_out=sm[:, 0:1]),
                      reads=[('xin', xi), 'sm'], writes=['xn', 'sm'])
                kb.op('act', lambda e: e.activation(out=sm[:, 1:2], in_=sm[:, 0:1], func=AF.Ln, bias=EPS, scale=1.0 / D),
                      reads=['sm'], writes=['sm'])
                kb.op('act', lambda e: e.activation(out=sm[:, 2:3], in_=sm[:, 1:2], func=AF.Exp, scale=-0.5), reads=['sm'], writes=['sm'])
                kb.op('dve', lambda e: e.tensor_scalar(out=xn[:], in0=xin[xi][:], scalar1=sm[:, 2:3], scalar2=None,
                                                       op0=ALU.mult), reads=[('xin', xi), 'sm'], writes=['xn'])
                bk, bkey = bank()
                bkb = bk[:].bitcast(BF16)
                for kc in range(8):
                    kb.op('pe', lambda e, kc=kc: e.transpose(bkb[:, kc * 128:(kc + 1) * 128], xn[:, kc * 128:(kc + 1) * 128], ident),
                          reads=['xn', 'cb'], writes=[bkey], inc=(kc == 7))
                kb.op('dve', lambda e: e.tensor_tensor(
                    out=hT[:, :, c * 128:(c + 1) * 128], in0=bkb.rearrange("p (k t) -> p k t", k=8),
                    in1=prm[:, P_NW:P_NW + 8].unsqueeze(2).to_broadcast([128, 8, 128]), op=ALU.mult),
                    reads=[bkey, 'prm'], writes=[('hT', c)])

        def mm_group(bk, bkey, wi, cj, n, rhs_tile, rhs_keys, kcs=8, w2=None):
            tot = kcs
            for kc in range(kcs):
                wt = wbuf[wi] if kc < 8 else wbuf[w2]
                wk_ = ('w', wi) if kc < 8 else ('w', w2)
                kb.op('pe', lambda e, kc=kc, wt=wt: e.matmul(bk[:, 0:n], lhsT=wt[:, kc % 8, cj * 128:(cj + 1) * 128],
                                                             rhs=rhs_tile[:, kc, 0:n], start=(kc == 0), stop=(kc == tot - 1)),
                      reads=[wk_] + rhs_keys, writes=[bkey], inc=(kc == tot - 1))

        def drain(gen):
            for _ in gen:
                pass

        def interleave(a, b):
            da = db = False
            while not (da and db):
                if not da:
                    try:
                        next(a)
                    except StopIteration:
                        da = True
                if not db:
                    try:
                        next(b)
                    except StopIteration:
                        db = True

        def chain(*gens):
            for g_ in gens:
                yield from g_

        def zipg(*gens):
            gens = list(gens)
            while gens:
                for g_ in list(gens):
                    try:
                        next(g_)
                        yield
                    except StopIteration:
                        gens.remove(g_)

        def run_tile(xrows, n, sample, seq, t0, yrows):
            nch = n // 128
            hkeys = [('hT', c) for c in range(nch)]
            first = (not sample) and t0 == 0
            last = (not sample) and (t0 + n == SEQ)
            if not sample:
                kb.dma('sp', rope[:, :, 0:n], rope_d[:, :, t0:t0 + n].rearrange("a p t -> p a t"), writes=['rope'])
                cosT = rope[:, 0, 0:n]
                sinT = rope[:, 1, 0:n]
                ropek = ['rope']
            else:
                cosT = cf[:, F_COSS:F_COSS + 128]
                sinT = cf[:, F_SINS:F_SINS + 128]
                ropek = ['cf']
            if STAGE < 1:
                return
            if sample:
                kb.dma('sp', hists, scT_d.rearrange("(c p) x -> p c x", p=128), writes=['S'])
            rmsnorm_to_hT(xrows, nch)
            if STAGE < 2:
                return

            for c in range(nch):
                bk, bkey = bank()
                for kc in range(8):
                    kb.op('pe', lambda e, kc=kc: e.matmul(bk[:, 0:288], lhsT=hT[:, kc, c * 128:(c + 1) * 128], rhs=wAt[:, kc, :],
                                                          start=(kc == 0), stop=(kc == 7)),
                          reads=[('hT', c), 'wAt'], writes=[bkey], inc=(kc == 7))
                vsrc = bk[:, 0:256].rearrange("p (g d) -> p g d", g=4)
                if SUB < 2:
                    continue
                kb.op('act', lambda e: e.activation(out=VE[:, 1 + c, :, 0:64], in_=vsrc, func=AF.Copy), reads=[bkey], writes=[('V', 1 + c)])
                if SUB < 3:
                    continue
                kb.op('act', lambda e: e.activation(out=VO[:, 1 + c, :, 64:128], in_=vsrc, func=AF.Copy), reads=[bkey], writes=[('V', 1 + c)])
                if SUB < 4:
                    continue
                kb.op('dve', lambda e: e.tensor_tensor(out=dtall[:, c, :], in0=bk[:, 256:288], in1=prm[:, P_DTB:P_DTB + 32], op=ALU.add),
                      reads=[bkey, 'prm'], writes=[('dtall', c)])
                ssd_prep(c, sample)
                yield
                if (last and c == nch - 1) or sample:
                    si = rot('st', 3)
                    kb.op('act', lambda e: e.activation(out=sttr[si][:, 0:256], in_=bk[:, 0:256], func=AF.Copy), reads=[bkey], writes=[('sttr', si)])
                    if not sample:
                        kb.dma('sp', vwp[seq], sttr[si][:, 0:256], reads=[('sttr', si)])
                    else:
                        for b in range(NB):
                            kb.dma('sp', vws[b, 120:128, :], sttr[si][b * 8:(b + 1) * 8, 0:256], reads=[('sttr', si)])

            yield 'front_done'
            def qk_chunk(bk, bkey, is_q, ci):
                i = rot('q')
                w_col = prm[:, P_WQ:P_WQ + 1] if is_q else prm[:, P_WK:P_WK + 1]
                RT = RTq if is_q else RTk
                RTn = 'RTq' if is_q else 'RTk'
                kb.op('act', lambda e: e.activation(out=qraw[i][:, 0:n], in_=bk[:, 0:n], func=AF.Copy), reads=[bkey], writes=[('qraw', i)])
                kb.op('act', lambda e: e.activation(out=qsq[i][:, 0:n], in_=bk[:, 0:n], func=AF.Square), reads=[bkey], writes=[('qsq', i)])
                kb.op('dve', lambda e: e.scalar_tensor_tensor(out=t1[i][:, 0:n], in0=bk[:, 0:n], scalar=w_col, in1=cosT,
                                                              op0=ALU.mult, op1=ALU.mult),
                      reads=[bkey, 'prm'] + ropek, writes=[('t1', i)])
                return qk_stages(is_q, ci, i, RT, RTn)

            def qk_stages(is_q, ci, i, RT, RTn):
                st8 = {}

                def s_ss():
                    st8['b2'] = bank()
                    b2, b2key = st8['b2']
                    kb.op('pe', lambda e: e.matmul(b2[:, 0:n], lhsT=blk, rhs=qsq[i][:, 0:n], start=True, stop=True),
                          reads=['cb', ('qsq', i)], writes=[b2key])

                def s_rot():
                    st8['b3'] = bank()
                    b3, b3key = st8['b3']
                    kb.op('pe', lambda e: e.matmul(b3[:, 0:n], lhsT=RT[:], rhs=qraw[i][:, 0:n], start=True, stop=True),
                          reads=[RTn, ('qraw', i)], writes=[b3key])

                def s_ln():
                    b2, b2key = st8['b2']
                    kb.op('act', lambda e: e.activation(out=rsq[i][:, 0:n], in_=b2[:, 0:n], func=AF.Ln, bias=EPS, scale=1.0 / 64),
                          reads=[b2key], writes=[('rsq', i)])

                def s_t2():
                    b3, b3key = st8['b3']
                    kb.op('dve', lambda e: e.tensor_tensor(out=t2[i][:, 0:n], in0=b3[:, 0:n], in1=sinT, op=ALU.mult),
                          reads=[b3key] + ropek, writes=[('t2', i)])

                def s_exp():
                    kb.op('act', lambda e: e.activation(out=rsq[i][:, 0:n], in_=rsq[i][:, 0:n], func=AF.Exp, scale=-0.5), reads=[('rsq', i)], writes=[('rsq', i)])

                def s_add():
                    kb.op('dve', lambda e: e.tensor_tensor(out=t1[i][:, 0:n], in0=t2[i][:, 0:n], in1=t1[i][:, 0:n], op=ALU.add),
                          reads=[('t2', i), ('t1', i)], writes=[('t1', i)])

                def s_fin():
                    if is_q:
                        kb.op('dve', lambda e: e.tensor_tensor(out=qT[:, ci, 0:n], in0=t1[i][:, 0:n], in1=rsq[i][:, 0:n], op=ALU.mult),
                              reads=[('t1', i), ('rsq', i)], writes=[('qT', ci)])
                    else:
                        kb.op('dve', lambda e: e.tensor_tensor(out=kTE[0:64, ci, 128:128 + n], in0=t1[i][0:64, 0:n], in1=rsq[i][0:64, 0:n], op=ALU.mult),
                              reads=[('t1', i), ('rsq', i)], writes=[('kT', ci)])
                        kb.op('dve', lambda e: e.tensor_tensor(out=kTO[64:128, ci, 128:128 + n], in0=t1[i][64:128, 0:n], in1=rsq[i][64:128, 0:n], op=ALU.mult),
                              reads=[('t1', i), ('rsq', i)], writes=[('kT', ci)])
                        if last or sample:
                            kb.op('dve', lambda e: e.tensor_tensor(out=kf32[:, ci, :], in0=t1[i][:, n - 128:n], in1=rsq[i][:, n - 128:n], op=ALU.mult),
                                  reads=[('t1', i), ('rsq', i)], writes=[('kf32', ci)])
                return [s_ss, s_rot, s_ln, s_t2, s_exp, s_add, s_fin]

            def xbc_chunk(bk, bkey, ci):
                i = rot('x', 4)
                nb_, T_ = (1, n) if not sample else (NB, TS)
                W_ = 3 + T_
                xr = xraw[i][:, 0:nb_ * W_].rearrange("p (b w) -> p b w", b=nb_)
                kb.op('act', lambda e: e.activation(out=xr[:, :, 3:W_], in_=bk[:, 0:n].rearrange("p (b t) -> p b t", b=nb_), func=AF.Copy),
                      reads=[bkey], writes=[('xrawb', i)])
                a3 = acc[i][:, 0:n].rearrange("p (b t) -> p b t", b=nb_)
                cw = lambda j: prm[:, P_CW + ci * 4 + j:P_CW + ci * 4 + j + 1]
                xk = [('xrawb', i), ('xrawh', i)]
                st = []
                st.append(lambda: kb.op('act', lambda e: e.activation(
                    out=xr[:, :, 0:3], in_=hist[:, ci, 0:nb_ * 3].rearrange("p (b j) -> p b j", b=nb_) if not sample
                    else hists[:, ci, :].rearrange("p (b j) -> p b j", b=nb_), func=AF.Copy),
                    reads=[('hist', ci)] if not sample else ['S'], writes=[('xrawh', i)]))
                st.append(lambda: kb.op('act', lambda e: e.activation(out=a3, in_=xr[:, :, 0:T_], func=AF.Copy, scale=cw(0)),
                                        reads=xk + ['prm'], writes=[('acc', i)]))
                if not sample:
                    st.append(lambda: kb.op('act', lambda e: e.activation(out=hist[:, ci, :], in_=xraw[i][:, n:n + 3], func=AF.Copy),
                                            reads=[('xrawb', i)], writes=[('hist', ci)]))
                for j in (1, 2, 3):
                    st.append(lambda j=j: kb.op('dve', lambda e: e.scalar_tensor_tensor(out=a3, in0=xr[:, :, j:j + T_], scalar=cw(j), in1=a3,
                                                                                        op0=ALU.mult, op1=ALU.add),
                                                reads=xk + ['prm', ('acc', i)], writes=[('acc', i)]))
                st.append(lambda: kb.op('act', lambda e: e.activation(out=xbcT[:, ci, 0:n], in_=acc[i][:, 0:n], func=AF.Silu,
                                                                      bias=prm[:, P_CB + ci:P_CB + ci + 1], scale=1.0),
                                        reads=[('acc', i), 'prm'], writes=[('xbc', ci)]))
                return st

            def lockstep(stage_lists):
                for k in range(max(len(l_) for l_ in stage_lists)):
                    for l_ in stage_lists:
                        if k < len(l_):
                            l_[k]()

            def stream(g0, ng, handler):
                pend = []
                for g in range(g0, g0 + ng):
                    wi = load_w(g)
                    for cj in range(4):
                        bk, bkey = bank()
                        mm_group(bk, bkey, wi, cj, n, hT, hkeys)
                        if len(pend) >= 2:
                            lockstep(pend)
                            del pend[:]
                        r_ = handler(bk, bkey, (g - g0) * 4 + cj, wi)
                        if r_:
                            pend.append(r_)
                        yield
                if pend:
                    lockstep(pend)
                    yield

            xpend = []

            def xbc_handler(bk, bkey, ci, wi):
                xpend.append(xbc_chunk(bk, bkey, ci))
                if ci % 4 == 3:
                    lockstep(xpend)
                    del xpend[:]
                if (last or sample) and ci % 4 == 3:
                    g4 = ci // 4
                    b2, b2key = bank()
                    for kc in range(8):
                        kb.op('pe', lambda e, kc=kc: e.matmul(b2[:, 0:512], lhsT=hT[:, kc, n - 128:n], rhs=wbuf[wi][:, kc, :],
                                                              start=(kc == 0), stop=(kc == 7)),
                              reads=[('hT', nch - 1), ('w', wi)], writes=[b2key], inc=(kc == 7))
                    si = rot('st', 3)
                    kb.op('act', lambda e: e.activation(out=sttr[si][:], in_=b2[:, 0:512], func=AF.Copy),
                          reads=[b2key], writes=[('sttr', si)])
                    if not sample:
                        kb.dma('sp', cvp[seq, :, g4 * 512:(g4 + 1) * 512], sttr[si][125:128, :], reads=[('sttr', si)])
                    else:
                        kb.dma('sp', scr[:, g4 * 512:(g4 + 1) * 512], sttr[si][:], reads=[('sttr', si)], writes=['scr'])

            def ssd_streams():
                yield from stream(G_ZS, 4, lambda bk, bkey, ci, wi: kb.op(
                    'act', lambda e: e.activation(out=zsT[:, ci, 0:n], in_=bk[:, 0:n], func=AF.Silu), reads=[bkey], writes=[('zs', ci)]))
                yield from stream(G_XBC, 6, xbc_handler)

            def attn_all():
                yield from attention_tile(nch, n, first)
                kb.op('pool', lambda e: e.tensor_copy(out=kTE[:, :, 0:128], in_=kTE[:, :, n:n + 128]),
                      reads=[('kT', g) for g in range(4)], writes=['kTprev'])
                kb.op('pool', lambda e: e.tensor_copy(out=kTO[:, :, 0:128], in_=kTO[:, :, n:n + 128]),
                      reads=[('kT', g) for g in range(4)], writes=['kTprev'])
                kb.op('pool', lambda e: e.tensor_copy(out=VE[:, 0, :, :], in_=VE[:, nch, :, :]), reads=[('V', nch)], writes=[('V', 0)])
                kb.op('pool', lambda e: e.tensor_copy(out=VO[:, 0, :, :], in_=VO[:, nch, :, :]), reads=[('V', nch)], writes=[('V', 0)])

            drain(ssd_streams())
            if sample:
                kb.dma('sp', cvs, scr.rearrange("(b t) c -> b t c", t=TS)[:, 5:8, :], reads=['scr'])

            def qk_za_attention():
                yield from stream(G_Q, 2, lambda bk, bkey, ci, wi: qk_chunk(bk, bkey, True, ci))
                yield from stream(G_K, 1, lambda bk, bkey, ci, wi: qk_chunk(bk, bkey, False, ci))
                yield from stream(G_ZA, 2, lambda bk, bkey, ci, wi: kb.op(
                    'act', lambda e: e.activation(out=zaT[:, ci, 0:n], in_=bk[:, 0:n], func=AF.Copy), reads=[bkey], writes=[('za', ci)]))
                kb.op('act', lambda e: e.activation(out=zaT[:, :, 0:n], in_=zaT[:, :, 0:n], func=AF.Silu),
                      reads=[('za', ci) for ci in range(8)], writes=[('za', ci) for ci in range(8)])
                if last or sample:
                    si = rot('st', 3)
                    for g in range(4):
                        bk, bkey = bank()
                        kb.op('pe', lambda e: e.transpose(bk[:, 0:128], kf32[:, g, :], identf),
                              reads=[('kf32', g), 'cf'], writes=[bkey])
                        kb.op('act', lambda e: e.activation(out=sttr[si][:, g * 64:(g + 1) * 64], in_=bk[:, 0:64], func=AF.Copy),
                              reads=[bkey], writes=[('sttr', si)])
                    if not sample:
                        kb.dma('sp', kwp[seq], sttr[si][:, 0:256], reads=[('sttr', si)])
                    else:
                        for b in range(NB):
                            kb.dma('sp', kws[b, 120:128, :], sttr[si][b * 8:(b + 1) * 8, 0:256], reads=[('sttr', si)])
                yield
                if not sample:
                    yield from zipg(attn_all(), gate_stream())
                else:
                    attention_sample()
                    yield

            if STAGE < 7:
                return
            if sample:
                ssd_sample_pre()
            def ssd_all():
                for c in range(nch):
                    yield from ssd_chunk(c, n, sample, first and c == 0)

            def gate_stream():
                yield from stream(G_GA, 4, lambda bk, bkey, ci, wi: kb.op(
                    'act', lambda e: e.activation(out=gT[:, ci, 0:n], in_=bk[:, 0:n], func=AF.Copy), reads=[bkey], writes=[('g', ci)]))

            def gate_sigmoid():
                kb.op('act', lambda e: e.activation(out=gT[:, :, 0:n], in_=gT[:, :, 0:n], func=AF.Sigmoid),
                      reads=[('g', ci) for ci in range(16)], writes=[('g', ci) for ci in range(16)])

            def attn_proj():
                ogk_ = [('za', i_) for i_ in range(8)]
                for g in range(2):
                    wi = load_w(G_PA + g)
                    for cj in range(4):
                        co = g * 4 + cj
                        bk, bkey = bank()
                        mm_group(bk, bkey, wi, cj, n, ogT, ogk_)
                        kb.op('dve', lambda e: e.tensor_tensor(out=mT[:, co, 0:n], in0=bk[:, 0:n], in1=gT[:, co, 0:n], op=ALU.mult),
                              reads=[bkey, ('g', co)], writes=[('qT', co)])
                        yield

            def mid_b():
                yield from qk_za_attention()
                gate_sigmoid()
                yield from attn_proj()

            if not sample:
                interleave(ssd_all(), mid_b())
            else:
                drain(qk_za_attention())
                drain(ssd_all())
                drain(gate_stream())
            if sample:
                gate_sigmoid()
            if last:
                for j in range(16):
                    bk, bkey = bank()
                    kb.op('pe', lambda e: e.transpose(bk[:, 0:128], S[:, j * 128:(j + 1) * 128], identf), reads=['S', 'cf'], writes=[bkey])
                    si = rot('st', 3)
                    kb.op('act', lambda e: e.activation(out=sttr[si][:, 0:128], in_=bk[:, 0:128], func=AF.Copy), reads=[bkey], writes=[('sttr', si)])
                    kb.dma('sp', ssp[seq, j * 128:(j + 1) * 128, :], sttr[si][:, 0:128], reads=[('sttr', si)])
                kb.op('pool', lambda e: e.memset(S[:], 0.0), writes=['S'])
                kb.op('pool', lambda e: e.memset(Sb[:], 0.0), writes=['Sb'])
                kb.op('pool', lambda e: e.memset(hist[:], 0.0), writes=[('hist', ci) for ci in range(24)])

            yield 'body_done'
            ogk = [('za', i_) for i_ in range(8)]
            ynk = [('yn', c, ci_) for c in range(nch) for ci_ in range(16)]
            if sample:
                yield from attn_proj()
            for g in range(2):
                wi = load_w(G_PS + 2 * g)
                bks = [bank() for _ in range(4)]
                for cj in range(4):
                    bk, bkey = bks[cj]
                    for kc in range(8):
                        kb.op('pe', lambda e, kc=kc: e.matmul(bk[:, 0:n], lhsT=wbuf[wi][:, kc, cj * 128:(cj + 1) * 128], rhs=ynT[:, kc, 0:n],
                                                              start=(kc == 0), stop=False), reads=[('w', wi)] + ynk, writes=[bkey], inc=False)
                wi2 = load_w(G_PS + 2 * g + 1)
                for cj in range(4):
                    co = g * 4 + cj
                    bk, bkey = bks[cj]
                    for kc in range(8):
                        kb.op('pe', lambda e, kc=kc: e.matmul(bk[:, 0:n], lhsT=wbuf[wi2][:, kc, cj * 128:(cj + 1) * 128], rhs=ynT[:, 8 + kc, 0:n],
                                                              start=False, stop=(kc == 7)), reads=[('w', wi2)] + ynk, writes=[bkey], inc=(kc == 7))
                    mi = rot('m1')
                    kb.op('dve', lambda e: e.tensor_tensor(out=t2[mi][:, 0:n], in0=bk[:, 0:n], in1=gT[:, 8 + co, 0:n], op=ALU.mult),
                          reads=[bkey, ('g', 8 + co)], writes=[('t2', mi)])
                    kb.op('dve', lambda e: e.tensor_tensor(out=mT[:, co, 0:n], in0=t2[mi][:, 0:n], in1=mT[:, co, 0:n], op=ALU.add),
                          reads=[('t2', mi), ('qT', co)], writes=[('qT', co)])
                yield
            mk = [('qT', co) for co in range(8)]
            ystores = []
            xis = []
            for c in range(nch):
                xi = rot('xin')
                kb.dma('sp', xin[xi][:], xrows[c * 128:(c + 1) * 128, :], writes=[('xin', xi)])
                xis.append(xi)
            for g in range(2):
                wo = load_w(G_WO + g)
                for c in range(nch):
                    xi = xis[c]
                    bk, bkey = bank()
                    for kc in range(8):
                        kb.op('pe', lambda e, kc=kc: e.matmul(bk[:, 0:512], lhsT=mT[:, kc, c * 128:(c + 1) * 128], rhs=wbuf[wo][:, kc, :],
                                                              start=(kc == 0), stop=(kc == 7)),
                              reads=mk + [('w', wo)], writes=[bkey], inc=(kc == 7))
                    yi = rot('st', 3)
                    kb.op('dve', lambda e: e.tensor_tensor(out=sttr[yi][:], in0=bk[:, 0:512], in1=xin[xi][:, g * 512:(g + 1) * 512], op=ALU.add),
                          reads=[bkey, ('xin', xi)], writes=[('sttr', yi)])
                    ystores.append((yrows[c * 128:(c + 1) * 128, g * 512:(g + 1) * 512], sttr[yi][:], ('sttr', yi)))
                    if len(ystores) >= 2:
                        o_, i_, k_ = ystores.pop(0)
                        kb.dma('sp', o_, i_, reads=[k_])
                    yield

            for o_, i_, k_ in ystores:
                kb.dma('sp', o_, i_, reads=[k_])

        def attention_chunk(c, n, noprev):
            q0 = c * 128
            kcur = 128 + q0
            kprev = q0
            terms = [(half, kt) for half in range(2) for kt in range(2) if not (noprev and kt == 0)]
            vkeys = [('V', c), ('V', c + 1)]

            def scores(i):
                g = i // 2
                bk, bkey = bank()
                kkeys = [('kT', g), 'kTprev']
                for ti, (half, kt) in enumerate(terms):
                    kTx = kTE if half == 0 else kTO
                    koff = kprev if kt == 0 else kcur
                    col = (half * 2 + kt) * 128
                    kb.op('pe', lambda e: e.matmul(bk[:, col:col + 128], lhsT=kTx[:, g, koff:koff + 128], rhs=qT[:, i, q0:q0 + 128],
                                                   start=True, stop=True),
                          reads=kkeys + [('qT', i)], writes=[bkey], inc=(ti == len(terms) - 1))
                pi = rot('p', 6)
                if noprev:
                    for half in range(2):
                        col = (half * 2 + 1) * 128
                        kb.op('act', lambda e: e.activation(out=Pt[pi][:, col:col + 128], in_=bk[:, col:col + 128], func=AF.Exp, scale=0.125),
                              reads=[bkey], writes=[('P', pi)])
                        kb.op('dve', lambda e: e.tensor_tensor(out=Pt[pi][:, col:col + 128], in0=Pt[pi][:, col:col + 128],
                                                               in1=cb[:, C_M4 + 128:C_M4 + 256], op=ALU.mult),
                              reads=[('P', pi), 'cb'], writes=[('P', pi)])
                else:
                    kb.op('act', lambda e: e.activation(out=Pt[pi][:], in_=bk[:, 0:512], func=AF.Exp, scale=0.125), reads=[bkey], writes=[('P', pi)])
                    kb.op('dve', lambda e: e.tensor_tensor(out=Pt[pi][:], in0=Pt[pi][:], in1=cb[:, C_M4:C_M4 + 512], op=ALU.mult),
                          reads=[('P', pi), 'cb'], writes=[('P', pi)])
                return pi

            def finish_stages(i, pi):
                g = i // 2
                st8 = {}

                def s_pe():
                    st8['bo'] = bank()
                    bo, bokey = st8['bo']
                    for ti, (half, kt) in enumerate(terms):
                        col = (half * 2 + kt) * 128
                        Vt = VE if half == 0 else VO
                        kb.op('pe', lambda e: e.matmul(bo[:, 0:128], lhsT=Vt[:, c + kt, g, :], rhs=Pt[pi][:, col:col + 128],
                                                       start=(ti == 0), stop=(ti == len(terms) - 1)),
                              reads=vkeys + [('P', pi)], writes=[bokey], inc=False)
                    for ti, (half, kt) in enumerate(terms):
                        col = (half * 2 + kt) * 128
                        on = onesE if half == 0 else onesO
                        kb.op('pe', lambda e: e.matmul(bo[:, 128:256], lhsT=on, rhs=Pt[pi][:, col:col + 128], start=(ti == 0), stop=False),
                              reads=['cb', ('P', pi)], writes=[bokey], inc=False)
                    kb.op('pe', lambda e: e.matmul(bo[:, 128:256], lhsT=sinkbc[:, i, :], rhs=cb[:, C_E0:C_E0 + 128], start=False, stop=True),
                          reads=['sinkbc', 'cb'], writes=[bokey])
                    st8['ri'] = rot('e', 3)

                def s_ln():
                    bo, bokey = st8['bo']
                    ri = st8['ri']
                    kb.op('act', lambda e: e.activation(out=Rr[ri][:], in_=bo[:, 128:256], func=AF.Ln), reads=[bokey], writes=[('Rr', ri)])

                def s_exp():
                    ri = st8['ri']
                    kb.op('act', lambda e: e.activation(out=Rr[ri][:], in_=Rr[ri][:], func=AF.Exp, scale=-1.0), reads=[('Rr', ri)], writes=[('Rr', ri)])

                def s_og():
                    bo, bokey = st8['bo']
                    ri = st8['ri']
                    kb.op('dve', lambda e: e.tensor_tensor(out=ogf[ri][:], in0=bo[:, 0:128], in1=Rr[ri][:], op=ALU.mult),
                          reads=[bokey, ('Rr', ri)], writes=[('ogf', ri)])

                def s_ogz():
                    ri = st8['ri']
                    kb.op('dve', lambda e: e.tensor_tensor(out=ogT[:, i, q0:q0 + 128], in0=ogf[ri][:], in1=zaT[:, i, q0:q0 + 128], op=ALU.mult),
                          reads=[('ogf', ri), ('za', i)], writes=[('za', i)])
                return [s_pe, s_ln, s_exp, s_og, s_ogz]

            return scores, finish_stages

        def attention_tile(nch, n, first):
            chunks = [attention_chunk(c, n, first and c == 0) for c in range(nch)]
            pend = [[] for _ in chunks]
            for i in range(8 + 2):
                if i < 8:
                    for ch, (sc_, _f) in enumerate(chunks):
                        pend[ch].append(sc_(i))
                    yield
                if i >= 2:
                    lockstep_g([f_(i - 2, pend[ch][i - 2]) for ch, (_s, f_) in enumerate(chunks)])
                    yield

        def lockstep_g(stage_lists):
            for k in range(max(len(l_) for l_ in stage_lists)):
                for l_ in stage_lists:
                    if k < len(l_):
                        l_[k]()

        def attention_sample():
            kb.dma('sp', kws[:, 0:120, :], ck_d[:, 8:128, :])
            kb.dma('sp', vws[:, 0:120, :], cv_d[:, 8:128, :])
            kb.op('pool', lambda e: e.memset(Eb[0][:], 0.0), writes=[('Eb', 0)])
            kb.op('pool', lambda e: e.memset(Eb[1][:], 0.0), writes=[('Eb', 1)])
            kb.op('pool', lambda e: e.memset(dxe[:], 0.0), writes=['dxe', 'dxeA', 'dxeB'])
            kkeys_ = [('Eb', 0), 'dxeA']
            vkeys_ = [('Eb', 1), 'dxeB']

            def load_cache(b):
                j = b % 2
                kb.dma('pool', kcE[j][0:64, :, :], ckT_d[b].rearrange("(g d) j -> d g j", g=4), writes=[kkeys_[j]])
                kb.dma('pool', kcO[j][64:128, :, :], ckT_d[b].rearrange("(g d) j -> d g j", g=4), writes=[kkeys_[j]])
                kb.dma('pool', vcE[j][:, :, 0:64], cv_d[b].rearrange("j (g d) -> j g d", g=4), writes=[vkeys_[j]])
                kb.dma('pool', vcO[j][:, :, 64:128], cv_d[b].rearrange("j (g d) -> j g d", g=4), writes=[vkeys_[j]])

            load_cache(0)
            for b in range(NB):
                j = b % 2
                if b + 1 < NB:
                    load_cache(b + 1)
                t0_, t1_ = b * TS, (b + 1) * TS
                bk, bkey = bank()
                for i in range(8):
                    g = i // 2
                    for half in range(2):
                        col = (i * 2 + half) * TS
                        kc_ = kcE[j] if half == 0 else kcO[j]
                        kn_ = kTE if half == 0 else kTO
                        kb.op('pe', lambda e: e.matmul(bk[:, col:col + TS], lhsT=kc_[:, g, :], rhs=qT[:, i, t0_:t1_], start=True, stop=True),
                              reads=[kkeys_[j], ('qT', i)], writes=[bkey], inc=False)
                        kb.op('pe', lambda e: e.matmul(bk[:, 128 + col:128 + col + TS], lhsT=kn_[:, g, 128:256], rhs=qT[:, i, t0_:t1_], start=True, stop=True),
                              reads=[('kT', g), ('qT', i)], writes=[bkey], inc=(i == 7 and half == 1))
                pi = rot('p')
                kb.op('act', lambda e: e.activation(out=Pt[pi][:, 0:256], in_=bk[:, 0:256], func=AF.Exp, scale=0.125), reads=[bkey], writes=[('P', pi)])
                kb.op('dve', lambda e: e.tensor_tensor(
                    out=Pt[pi][:, 0:128].rearrange("p (a t) -> p a t", t=TS), in0=Pt[pi][:, 0:128].rearrange("p (a t) -> p a t", t=TS),
                    in1=cb[:, C_MSC:C_MSC + TS].unsqueeze(1).to_broadcast([128, 16, TS]), op=ALU.mult), reads=[('P', pi), 'cb'], writes=[('P', pi)])
                kb.op('dve', lambda e: e.tensor_tensor(
                    out=Pt[pi][:, 128:256].rearrange("p (a t) -> p a t", t=TS), in0=Pt[pi][:, 128:256].rearrange("p (a t) -> p a t", t=TS),
                    in1=cb[:, C_MSN + t0_:C_MSN + t1_].unsqueeze(1).to_broadcast([128, 16, TS]), op=ALU.mult), reads=[('P', pi), 'cb'], writes=[('P', pi)])
                bo, bokey = bank()
                for i in range(8):
                    g = i // 2
                    ce, co = (i * 2) * TS, (i * 2 + 1) * TS
                    kb.op('pe', lambda e: e.matmul(bo[:, i * TS:(i + 1) * TS], lhsT=vcE[j][:, g, :], rhs=Pt[pi][:, ce:ce + TS], start=True, stop=False),
                          reads=[vkeys_[j], ('P', pi)], writes=[bokey], inc=False)
                    kb.op('pe', lambda e: e.matmul(bo[:, i * TS:(i + 1) * TS], lhsT=vcO[j][:, g, :], rhs=Pt[pi][:, co:co + TS], start=False, stop=False),
                          reads=[vkeys_[j], ('P', pi)], writes=[bokey], inc=False)
                    kb.op('pe', lambda e: e.matmul(bo[:, i * TS:(i + 1) * TS], lhsT=VE[:, 1, g, :], rhs=Pt[pi][:, 128 + ce:128 + ce + TS], start=False, stop=False),
                          reads=[('V', 1), ('P', pi)], writes=[bokey], inc=False)
                    kb.op('pe', lambda e: e.matmul(bo[:, i * TS:(i + 1) * TS], lhsT=VO[:, 1, g, :], rhs=Pt[pi][:, 128 + co:128 + co + TS], start=False, stop=True),
                          reads=[('V', 1), ('P', pi)], writes=[bokey], inc=False)
                kb.op('pe', lambda e: e.matmul(bo[:, 128:256], lhsT=ones, rhs=Pt[pi][:, 0:128], start=True, stop=False), reads=['cb', ('P', pi)], writes=[bokey], inc=False)
                kb.op('pe', lambda e: e.matmul(bo[:, 128:256], lhsT=ones, rhs=Pt[pi][:, 128:256], start=False, stop=False), reads=['cb', ('P', pi)], writes=[bokey], inc=False)
                kb.op('pe', lambda e: e.matmul(bo[:, 128:256], lhsT=cb[:, C_E0:C_E0 + 128], rhs=sinkx[:], start=False, stop=True), reads=['cb', 'sinkx'], writes=[bokey])
                ri = rot('e')
                kb.op('act', lambda e: e.activation(out=Rr[ri][:], in_=bo[:, 128:256], func=AF.Ln), reads=[bokey], writes=[('Rr', ri)])
                kb.op('act', lambda e: e.activation(out=Rr[ri][:], in_=Rr[ri][:], func=AF.Exp, scale=-1.0), reads=[('Rr', ri)], writes=[('Rr', ri)])
                R4 = Rr[ri][:].rearrange("p (i h t) -> p i h t", i=8, h=2)
                O3 = bo[:, 0:64].rearrange("p (i t) -> p i t", i=8)
                og3 = ogf[ri][:, 0:64].rearrange("p (i t) -> p i t", i=8)
                kb.op('dve', lambda e: e.tensor_tensor(out=og3[0:64], in0=O3[0:64], in1=R4[0:64, :, 0, :], op=ALU.mult), reads=[bokey, ('Rr', ri)], writes=[('ogf', ri)])
                kb.op('dve', lambda e: e.tensor_tensor(out=og3[64:128], in0=O3[64:128], in1=R4[64:128, :, 1, :], op=ALU.mult), reads=[bokey, ('Rr', ri)], writes=[('ogf', ri)])
                kb.op('dve', lambda e: e.tensor_tensor(out=ogT[:, :, t0_:t1_], in0=og3, in1=zaT[:, :, t0_:t1_], op=ALU.mult),
                      reads=[('ogf', ri)] + [('za', i) for i in range(8)], writes=[('za', i) for i in range(8)])
            kb.op('pool', lambda e: e.memset(dxe[:, 0:2], 0.0), reads=[], writes=['dxe', 'dxeA', 'dxeB'])

        def ssd_prep(c, sample):
            Lm = cb[:, C_L:C_L + 128] if not sample else cb[:, C_LB:C_LB + 128]
            dtt, av, ahi, alo, atmp, dte, cdb = (t_[:, c, :] for t_ in (dtt3, av3, ahi3, alo3, atmp3, dte3, cdb3))
            K = lambda nm: (nm, c)
            kb.op('act', lambda e: e.activation(out=dtt, in_=dtall[:, c, :], func=AF.Exp), reads=[('dtall', c)], writes=[K('dtt')])
            kb.op('act', lambda e: e.activation(out=dtt, in_=dtt, func=AF.Ln, bias=1.0, scale=1.0), reads=[K('dtt')], writes=[K('dtt')])
            kb.op('dve', lambda e: e.tensor_tensor(out=av, in0=dtt, in1=Aneg[:], op=ALU.mult), reads=[K('dtt'), 'Aneg'], writes=[K('av')])
            kb.op('dve', lambda e: e.tensor_copy(out=ahi, in_=av), reads=[K('av')], writes=[K('ahi')])
            kb.op('dve', lambda e: e.tensor_tensor(out=atmp, in0=av, in1=ahi, op=ALU.subtract), reads=[K('av'), K('ahi')], writes=[K('atmp')])
            kb.op('dve', lambda e: e.tensor_copy(out=alo, in_=atmp), reads=[K('atmp')], writes=[K('alo')])
            bk, bkey = bank()
            kb.op('pe', lambda e: e.matmul(bk[:, 0:32], lhsT=Lm, rhs=ahi, start=True, stop=False), reads=['cb', K('ahi')], writes=[bkey], inc=False)
            kb.op('pe', lambda e: e.matmul(bk[:, 0:32], lhsT=Lm, rhs=alo, start=False, stop=True), reads=['cb', K('alo')], writes=[bkey], inc=False)
            kb.op('pe', lambda e: e.matmul(bk[:, 32:64], lhsT=ones, rhs=ahi, start=True, stop=False), reads=['cb', K('ahi')], writes=[bkey], inc=False)
            kb.op('pe', lambda e: e.matmul(bk[:, 32:64], lhsT=ones, rhs=alo, start=False, stop=True), reads=['cb', K('alo')], writes=[bkey])
            kb.op('act', lambda e: e.activation(out=dte, in_=bk[:, 0:32], func=AF.Exp), reads=[bkey], writes=[K('dte')])
            kb.op('act', lambda e: e.activation(out=cdb, in_=bk[:, 32:64], func=AF.Exp), reads=[bkey], writes=[K('cdb')])
            kb.op('dve', lambda e: e.tensor_tensor(out=dte, in0=dte, in1=dtt, op=ALU.mult), reads=[K('dte'), K('dtt')], writes=[K('dte')])

        def ssd_chunk(c, n, sample, fresh):
            q0 = c * 128
            Um = cb[:, C_U:C_U + 128] if not sample else cb[:, C_UB:C_UB + 128]
            Lm = cb[:, C_L:C_L + 128] if not sample else cb[:, C_LB:C_LB + 128]
            dtt, ahi, alo, dte, cdb = (t_[:, c, :] for t_ in (dtt3, ahi3, alo3, dte3, cdb3))
            kdtt, kahi, kalo, kdte, kcdb = (('dtt', c), ('ahi', c), ('alo', c), ('dte', c), ('cdb', c))
            kb.op('act', lambda e: e.activation(out=abcH[:], in_=ahi.unsqueeze(2).to_broadcast([128, 32, 64]), func=AF.Copy), reads=[kahi], writes=['abcH'])
            kb.op('act', lambda e: e.activation(out=abcL[:], in_=alo.unsqueeze(2).to_broadcast([128, 32, 64]), func=AF.Copy), reads=[kalo], writes=['abcL'])
            yield
            for half in range(2):
                bt, btkey = bank()
                btb = bt[:].bitcast(BF16)
                for j in range(8):
                    ci = half * 8 + j
                    kb.op('pe', lambda e, j=j, ci=ci: e.transpose(btb[:, j * 128:(j + 1) * 128], xbcT[:, ci, q0:q0 + 128], ident),
                          reads=[('xbc', ci), 'cb'], writes=[btkey], inc=(j == 7))
                src = btb.rearrange("p (i e d) -> p i e d", i=8, e=2)
                hs = half * 16
                dtE = dtt[:, hs:hs + 16:2].unsqueeze(2).to_broadcast([128, 8, 64])
                dtO = dtt[:, hs + 1:hs + 16:2].unsqueeze(2).to_broadcast([128, 8, 64])
                kb.op('dve', lambda e: e.tensor_tensor(out=dxE[:, half * 8:half * 8 + 8, 0:64], in0=src[:, :, 0, :], in1=dtE, op=ALU.mult),
                      reads=[btkey, kdtt], writes=['dxE'])
                kb.op('dve', lambda e: e.tensor_tensor(out=dxO[:, half * 8:half * 8 + 8, 64:128], in0=src[:, :, 1, :], in1=dtO, op=ALU.mult),
                      reads=[btkey, kdtt], writes=['dxO'])
                kb.op('dve', lambda e: e.tensor_tensor(
                    out=dxe[:, half * 1024:(half + 1) * 1024].rearrange("p (h d) -> p h d", h=16),
                    in0=btb.rearrange("p (h d) -> p h d", h=16),
                    in1=dte[:, hs:hs + 16].unsqueeze(2).to_broadcast([128, 16, 64]), op=ALU.mult),
                    reads=[btkey, kdte], writes=['dxe'])
                yield
            bt, btkey = bank()
            btb = bt[:].bitcast(BF16)
            for g in range(4):
                kb.op('pe', lambda e, g=g: e.transpose(btb[:, g * 128:(g + 1) * 128], xbcT[:, 16 + g, q0:q0 + 128], ident),
                      reads=[('xbc', 16 + g), 'cb'], writes=[btkey], inc=(g == 3))
            kb.op('act', lambda e: e.activation(out=Btok[:], in_=btb[:, 0:512], func=AF.Copy), reads=[btkey], writes=['Btok'])

            yield
            def prep(g):
                BT = xbcT[:, 16 + g, q0:q0 + 128]
                CT = xbcT[:, 20 + g, q0:q0 + 128]
                bk, bkey = bank()
                kb.op('pe', lambda e: e.matmul(bk[:, 0:128], lhsT=BT, rhs=CT, start=True, stop=True),
                      reads=[('xbc', 16 + g), ('xbc', 20 + g)], writes=[bkey])
                ei = rot('se')
                kb.op('dve', lambda e: e.tensor_tensor(out=cbm[ei][:], in0=bk[:, 0:128], in1=Um, op=ALU.mult), reads=[bkey, 'cb'], writes=[('cbm', ei)])
                Ub = Um.unsqueeze(1).to_broadcast([128, 8, 128])
                kb.op('dve', lambda e: e.tensor_tensor(out=AUh[ei][:], in0=Ub, in1=ahi[:, g * 8:g * 8 + 8].unsqueeze(2).to_broadcast([128, 8, 128]), op=ALU.mult),
                      reads=['cb', kahi], writes=[('AUh', ei)])
                kb.op('dve', lambda e: e.tensor_tensor(out=AUl[ei][:], in0=Ub, in1=alo[:, g * 8:g * 8 + 8].unsqueeze(2).to_broadcast([128, 8, 128]), op=ALU.mult),
                      reads=['cb', kalo], writes=[('AUl', ei)])
                return ei

            def seg(ei):
                for hh in range(2):
                    bs, bskey = bank()
                    kb.op('pe', lambda e: e.matmul(bs[:, 0:512], lhsT=Lm, rhs=AUh[ei][:, hh * 4:hh * 4 + 4, :], start=True, stop=False),
                          reads=['cb', ('AUh', ei)], writes=[bskey], inc=False)
                    kb.op('pe', lambda e: e.matmul(bs[:, 0:512], lhsT=Lm, rhs=AUl[ei][:, hh * 4:hh * 4 + 4, :], start=False, stop=True),
                          reads=['cb', ('AUl', ei)], writes=[bskey])
                    kb.op('act', lambda e: e.activation(out=Eb[ei][:, hh * 4:hh * 4 + 4, :], in_=bs[:, 0:512].rearrange("p (h l) -> p h l", h=4), func=AF.Exp),
                          reads=[bskey], writes=[('Eb', ei)])

            def ebcbm(ei):
                kb.op('dve', lambda e: e.tensor_tensor(out=Eb[ei][:], in0=Eb[ei][:], in1=cbm[ei][:].unsqueeze(1).to_broadcast([128, 8, 128]), op=ALU.mult),
                      reads=[('Eb', ei), ('cbm', ei)], writes=[('Eb', ei)])

            eis = [prep(0)]
            seg(eis[0])
            eis.append(prep(1))
            ebcbm(eis[0])
            yield
            for g in range(4):
                CT = xbcT[:, 20 + g, q0:q0 + 128]
                ei = eis[g]
                if g + 1 < 4:
                    seg(eis[g + 1])
                if g + 2 < 4:
                    eis.append(prep(g + 2))
                yield
                stage_lists = []
                for pr in range(4):
                    ci = g * 4 + pr
                    he, ho = 2 * pr, 2 * pr + 1
                    by, bykey = bank()
                    kb.op('pe', lambda e: e.matmul(by[:, 0:128], lhsT=dxE[:, ci, :], rhs=Eb[ei][:, he, :], start=True, stop=False),
                          reads=['dxE', ('Eb', ei)], writes=[bykey], inc=False)
                    kb.op('pe', lambda e: e.matmul(by[:, 0:128], lhsT=dxO[:, ci, :], rhs=Eb[ei][:, ho, :], start=False, stop=True),
                          reads=['dxO', ('Eb', ei)], writes=[bykey], inc=False)
                    kb.op('pe', lambda e: e.matmul(by[:, 256:384], lhsT=abcH[:, ci * 2:ci * 2 + 2, :], rhs=Um, start=True, stop=False),
                          reads=['abcH', 'cb'], writes=[bykey], inc=False)
                    kb.op('pe', lambda e: e.matmul(by[:, 256:384], lhsT=abcL[:, ci * 2:ci * 2 + 2, :], rhs=Um, start=False, stop=True),
                          reads=['abcL', 'cb'], writes=[bykey], inc=False)
                    if not sample:
                        kb.op('pe', lambda e: e.matmul(by[:, 128:256], lhsT=Sb[:, ci * 128:(ci + 1) * 128], rhs=CT, start=True, stop=True),
                              reads=['Sb', ('xbc', 20 + g)], writes=[bykey])
                        yoff_ap, yoff_keys = by[:, 128:256], [bykey]
                    else:
                        kb.op('pe', lambda e: e.matmul(by[:, 128:136], lhsT=ones, rhs=ones[:, 0:8], start=True, stop=True), reads=['cb'], writes=[bykey])
                        yoff_ap, yoff_keys = S[:, ci * 128:(ci + 1) * 128], ['S']
                    yi = rot('y', 4)

                    def mk(ci=ci, by=by, bykey=bykey, yi=yi, yoff_ap=yoff_ap, yoff_keys=yoff_keys):
                        return [
                            lambda: kb.op('act', lambda e: e.activation(out=ea[yi][:], in_=by[:, 256:384], func=AF.Exp), reads=[bykey], writes=[('ea', yi)]),
                            lambda: kb.op('dve', lambda e: e.tensor_tensor(out=yt[yi][:], in0=yoff_ap, in1=ea[yi][:], op=ALU.mult),
                                          reads=yoff_keys + [('ea', yi)], writes=[('yt', yi)]),
                            lambda: kb.op('dve', lambda e: e.tensor_tensor(out=yt[yi][:], in0=by[:, 0:128], in1=yt[yi][:], op=ALU.add),
                                          reads=[bykey, ('yt', yi)], writes=[('yt', yi)]),
                            lambda: kb.op('dve', lambda e: e.scalar_tensor_tensor(out=yt[yi][:], in0=xbcT[:, ci, q0:q0 + 128], scalar=prm[:, P_D + ci:P_D + ci + 1],
                                                                                  in1=yt[yi][:], op0=ALU.mult, op1=ALU.add),
                                          reads=[('xbc', ci), 'prm', ('yt', yi)], writes=[('yt', yi)]),
                            lambda: kb.op('dve', lambda e: e.tensor_tensor(out=yb[:, ci, :], in0=yt[yi][:], in1=zsT[:, ci, q0:q0 + 128], op=ALU.mult),
                                          reads=[('yt', yi), ('zs', ci)], writes=[('yb', ci)]),
                            lambda: kb.op('act', lambda e: e.activation(out=ysq[:, ci, :], in_=yb[:, ci, :], func=AF.Square), reads=[('yb', ci)], writes=[('ysq', ci)]),
                        ]
                    stage_lists.append(mk())
                for k in range(6):
                    for l_ in stage_lists:
                        l_[k]()
                if g + 1 < 4:
                    ebcbm(eis[g + 1])
                yield
                if not sample:
                    bs, bskey = bank()
                    kb.op('pe', lambda e: e.matmul(bs[:, 0:512], lhsT=Btok[:, g * 128:(g + 1) * 128], rhs=dxe[:, g * 512:(g + 1) * 512], start=True, stop=True),
                          reads=['Btok', 'dxe'], writes=[bskey])
                    Sg = S[:, g * 512:(g + 1) * 512]
                    kb.op('dve', lambda e: e.tensor_tensor(out=Sg.rearrange("p (h d) -> p h d", h=8), in0=Sg.rearrange("p (h d) -> p h d", h=8),
                                                           in1=cdb[:, g * 8:g * 8 + 8].unsqueeze(2).to_broadcast([128, 8, 64]), op=ALU.mult),
                          reads=['S', kcdb], writes=['S'])
                    kb.op('dve', lambda e: e.tensor_tensor(out=Sg, in0=Sg, in1=bs[:, 0:512], op=ALU.add), reads=['S', bskey], writes=['S'])
                    kb.op('act', lambda e: e.activation(out=Sb[:, g * 512:(g + 1) * 512], in_=Sg, func=AF.Copy), reads=['S'], writes=['Sb'])
                yield
            if sample:
                ssd_sample_state()
            bn, bnkey = bank()
            for g in range(4):
                for pr in range(4):
                    ci = g * 4 + pr
                    kb.op('pe', lambda e: e.matmul(bn[:, g * 128:(g + 1) * 128], lhsT=ones, rhs=ysq[:, ci, :], start=(pr == 0), stop=(pr == 3)),
                          reads=['cb', ('ysq', ci)], writes=[bnkey], inc=(g == 3 and pr == 3))
            kb.op('act', lambda e: e.activation(out=rsn[:], in_=bn[:, 0:512].rearrange("p (g l) -> p g l", g=4), func=AF.Ln, bias=EPS, scale=1.0 / 512),
                  reads=[bnkey], writes=['rsn'])
            kb.op('act', lambda e: e.activation(out=rsn[:], in_=rsn[:], func=AF.Exp, scale=-0.5), reads=['rsn'], writes=['rsn'])
            for ci in range(16):
                kb.op('dve', lambda e, ci=ci: e.scalar_tensor_tensor(out=ynT[:, ci, q0:q0 + 128], in0=yb[:, ci, :], scalar=prm[:, P_NWS + ci:P_NWS + ci + 1],
                                                                     in1=rsn[:, ci // 4, :], op0=ALU.mult, op1=ALU.mult),
                      reads=[('yb', ci), 'prm', 'rsn'], writes=[('yn', c, ci)])
            yield

        def ssd_sample_pre():
            yoffs = S[:].rearrange("p (c l) -> p c l", c=16)
            h0bufs = [ynT[:, :, 128:256], zsT[:, :, 128:256]]
            for b in range(NB):
                j = b % 2
                kb.dma('pool', h0bufs[j], ssT_d[b].rearrange("n (c m) -> n c m", c=16), writes=[('h0T', j)])
                bk, bkey = bank()
                for ci in range(16):
                    kb.op('pe', lambda e: e.matmul(bk[:, ci * TS:(ci + 1) * TS], lhsT=h0bufs[j][:, ci, :], rhs=xbcT[:, 20 + ci // 4, b * TS:(b + 1) * TS],
                                                   start=True, stop=True), reads=[('h0T', j), ('xbc', 20 + ci // 4)], writes=[bkey], inc=(ci == 15))
                kb.op('act', lambda e: e.activation(out=yoffs[:, :, b * TS:(b + 1) * TS], in_=bk[:, 0:128].rearrange("p (c t) -> p c t", c=16), func=AF.Copy),
                      reads=[bkey], writes=['S'])

        def ssd_sample_state():
            bk, bkey = bank()
            for ci in range(16):
                kb.op('pe', lambda e: e.matmul(bk[:, ci * 16:(ci + 1) * 16], lhsT=abcH[:, ci * 2:ci * 2 + 2, :], rhs=cb[:, C_SEL:C_SEL + 16], start=True, stop=False),
                      reads=['abcH', 'cb'], writes=[bkey], inc=False)
                kb.op('pe', lambda e: e.matmul(bk[:, ci * 16:(ci + 1) * 16], lhsT=abcL[:, ci * 2:ci * 2 + 2, :], rhs=cb[:, C_SEL:C_SEL + 16], start=False, stop=True),
                      reads=['abcL', 'cb'], writes=[bkey], inc=(ci == 15))
            kb.op('act', lambda e: e.activation(out=cdall[:], in_=bk[:, 0:256].rearrange("p (c b) -> p c b", c=16), func=AF.Exp), reads=[bkey], writes=['cdall'])
            kb.op('pool', lambda e: e.memset(S[:, 0:1], 0.0), writes=['S', ('Su', 0), ('Su', 1)])
            kb.op('pool', lambda e: e.memset(Sb[:, 0:2], 0.0), writes=['Sb', ('Su', 2)])
            units = [
                (gT[:].rearrange("p a b -> p (a b)")[:, 0:2048].bitcast(F32), [('g', c_) for c_ in range(8)]),
                (gT[:].rearrange("p a b -> p (a b)")[:, 2048:4096].bitcast(F32), [('g', c_) for c_ in range(8, 16)]),
                (qT[:].rearrange("p a b -> p (a b)")[:, 0:2048].bitcast(F32), [('qT', c_) for c_ in range(8)]),
                (S[:, 0:1024], [('Su', 0)]),
                (S[:, 1024:2048], [('Su', 1)]),
                (Sb[:, 0:2048].bitcast(F32), [('Su', 2)]),
            ]
            def load_state(b):
                res = []
                for hh in range(2):
                    uap, ukeys = units[(2 * b + hh) % 6]
                    h0 = uap.rearrange("p (c n) -> p c n", c=8)
                    kb.dma('sp', h0, ss_d[b, hh * 1024:(hh + 1) * 1024, :].rearrange("(c p) n -> p c n", p=128), writes=ukeys)
                    res.append((h0, ukeys))
                return res

            loaded = {0: load_state(0), 1: load_state(1)}
            for b in range(NB):
                bi = rot('bm')
                kb.op('dve', lambda e: e.tensor_scalar(out=Bm[bi][:], in0=Btok[:], scalar1=cf[:, F_SEL + b:F_SEL + b + 1], scalar2=None, op0=ALU.mult),
                      reads=['Btok', 'cf'], writes=[('P', bi)])
                if b + 2 < NB:
                    loaded[b + 2] = load_state(b + 2)
                for hh in range(2):
                    h0, ukeys = loaded[b][hh]
                    for q4 in range(2):
                        c0 = hh * 8 + q4 * 4
                        bs_, bskey = bank()
                        for k4 in range(4):
                            ci = c0 + k4
                            kb.op('pe', lambda e: e.matmul(bs_[:, k4 * 128:(k4 + 1) * 128], lhsT=dxe[:, ci * 128:(ci + 1) * 128],
                                                           rhs=Bm[bi][:, (ci // 4) * 128:(ci // 4 + 1) * 128], start=True, stop=True),
                                  reads=['dxe', ('P', bi)], writes=[bskey], inc=(k4 == 3))
                        hv = h0[:, q4 * 4:q4 * 4 + 4, :]
                        kb.op('dve', lambda e: e.tensor_tensor(out=hv, in0=hv, in1=cdall[:, c0:c0 + 4, b:b + 1].to_broadcast([128, 4, 128]), op=ALU.mult),
                              reads=ukeys + ['cdall'], writes=ukeys)
                        kb.op('dve', lambda e: e.tensor_tensor(out=hv, in0=hv, in1=bs_[:, 0:512].rearrange("p (c n) -> p c n", c=4), op=ALU.add),
                              reads=ukeys + [bskey], writes=ukeys)
                    kb.dma('act', sss[b, hh * 1024:(hh + 1) * 1024, :].rearrange("(c p) n -> p c n", p=128), h0, reads=ukeys)

        dtall = sb("dtall", [128, NT // 128, 32], F32)
        sinkx = sb("sinkx", [128, 16, TS])
        cdall = sb("cdall", [128, 16, 16], F32)
        dxe3 = dxe[:].rearrange("p (a l) -> p a l", a=16)
        kcE = [Eb[0][:, 0:4, :], dxe3[:, 0:4, :]]
        kcO = [Eb[0][:, 4:8, :], dxe3[:, 4:8, :]]
        vcE = [Eb[1][:, 0:4, :], dxe3[:, 8:12, :]]
        vcO = [Eb[1][:, 4:8, :], dxe3[:, 12:16, :]]
        Bm = Pt
        rr['bm'] = 0
        kb.op('dve', lambda e: e.tensor_copy(out=sinkx[:], in_=esink[:].unsqueeze(2).to_broadcast([128, 16, TS])), reads=['esink'], writes=['sinkx'])
        hists = S[:, 0:24 * NB * 3].rearrange("p (c x) -> p c x", c=24)

        tiles = []
        for s_ in range(NSEQ):
            for t0 in range(0, SEQ, NT):
                r0 = s_ * SEQ + t0
                tiles.append((xp[r0:r0 + NT, :], NT, False, s_, t0, yp[r0:r0 + NT, :]))
        if max_tiles is not None:
            tiles = tiles[:max_tiles]
        if do_sample:
            tiles.append((xs, 128, True, 0, 0, ys))
        wstate['total'] = NG * len(tiles)
        gens = [run_tile(*a_) for a_ in tiles]

        def run_until(g_, tag):
            for v in g_:
                if v == tag:
                    return

        if gens:
            run_until(gens[0], 'front_done')
        for i, cur in enumerate(gens):
            run_until(cur, 'body_done')
            nxt = gens[i + 1] if i + 1 < len(gens) else None
            done_c, done_n = False, nxt is None
            while not (done_c and done_n):
                if not done_c:
                    try:
                        next(cur)
                    except StopIteration:
                        done_c = True
                if not done_n:
                    try:
                        if next(nxt) == 'front_done':
                            done_n = True
                    except StopIteration:
                        done_n = True
        kb.finish()
        print("instructions:", kb.ninst, "waits:", kb.nwait, "per-engine:", kb.cnt)
    return nc


def _consts():
    cb = np.zeros((128, NCB), np.float32)
    p = np.arange(128)
    cb[:, C_ID:C_ID + 128] = np.eye(128)
    cb[:, C_ONES:C_ONES + 128] = 1.0
    cb[:, C_ONESE:C_ONESE + 64] = 1.0
    cb[:, C_ONESO + 64:C_ONESO + 128] = 1.0
    cb[:, C_BLK:C_BLK + 128] = (p[:, None] // 64 == p[None, :] // 64)
    rt = np.zeros((128, 128), np.float32)
    for k in range(128):
        d = k % 64
        if d < 8:
            rt[k, k + 8] = 1.0
        elif d < 16:
            rt[k, k - 8] = -1.0
    cb[:, C_RT:C_RT + 128] = rt
    key, q = p[:, None], p[None, :]
    mp = (key > q).astype(np.float32)
    mc = (q >= key).astype(np.float32)
    cb[:, C_M4:C_M4 + 512] = np.concatenate([mp, mc, mp, mc], 1)
    cb[:, C_U:C_U + 128] = (key <= q)
    cb[:, C_L:C_L + 128] = (key > q)
    same = (key // TS == q // TS)
    cb[:, C_UB:C_UB + 128] = (key <= q) & same
    cb[:, C_LB:C_LB + 128] = (key > q) & same
    cb[:, C_SEL:C_SEL + 16] = (p[:, None] // TS == np.arange(16)[None, :])
    cb[:, C_MSC:C_MSC + 8] = (p[:, None] > np.arange(8)[None, :])
    bq = np.arange(128)[None, :]
    cb[:, C_MSN:C_MSN + 128] = ((p[:, None] // TS) == (bq // TS)) & ((p[:, None] % TS) <= (bq % TS))
    cb[0, C_E0:C_E0 + 128] = 1.0
    cf = np.zeros((128, NCF), np.float32)
    cf[:, F_ID:F_ID + 128] = np.eye(128)
    half = 8
    inv_freq = np.power(np.float32(500000.0), -np.arange(half, dtype=np.float32) * np.float32(2.0 / 16)).astype(np.float32)

    def tables(pos):
        invf64 = np.power(500000.0, -np.arange(half, dtype=np.float64) * (2.0 / 16))
        ang = pos.astype(np.float64)[None, :] * invf64[:, None]
        cos8, sin8 = np.cos(ang).astype(np.float32), np.sin(ang).astype(np.float32)
        ct = np.ones((128, pos.shape[0]), np.float32)
        st = np.zeros((128, pos.shape[0]), np.float32)
        for pp in range(128):
            d = pp % 64
            if d < 16:
                ct[pp] = cos8[d % 8]
                st[pp] = sin8[d % 8]
        return ct, st
    cp, sp_ = tables(np.arange(SEQ))
    rope = np.stack([cp, sp_], 0)
    cs, ss_ = tables(PAST + (np.arange(128) % TS))
    cf[:, F_COSS:F_COSS + 128] = cs
    cf[:, F_SEL:F_SEL + 16] = (np.arange(128)[:, None] // TS == np.arange(16)[None, :])
    cf[:, F_SINS:F_SINS + 128] = ss_
    return cb, cf, rope


def _grp(w, rows_kc, col0):
    blk_ = w[:, col0:col0 + 512]
    return np.ascontiguousarray(blk_.reshape(rows_kc, 128, 512).transpose(1, 0, 2))


def _layout_weights(w_in, w_attn, w_ssd, w_out):
    cols = list(range(OFF_Q, OFF_Q + 1024))
    for g in range(4):
        kc = list(range(OFF_K + g * 64, OFF_K + (g + 1) * 64))
        cols += kc + kc
    cols += list(range(OFF_ZA, OFF_ZA + 1024))
    cols += list(range(OFF_ZS, OFF_ZS + 2048))
    cols += list(range(OFF_XBC, OFF_XBC + 3072))
    cols += list(range(OFF_G, OFF_G + 2048))
    wp = w_in[:, cols]
    groups = [_grp(wp, 8, g * 512) for g in range(19)]
    groups += [_grp(w_attn, 8, g * 512) for g in range(2)]
    for g in range(2):
        groups += [_grp(w_ssd[0:1024], 8, g * 512), _grp(w_ssd[1024:2048], 8, g * 512)]
    groups += [_grp(w_out, 8, g * 512) for g in range(2)]
    wall = np.stack(groups, 0)
    colsA = list(range(OFF_V, OFF_V + 256)) + list(range(OFF_DT, OFF_DT + 32))
    wA = np.ascontiguousarray(w_in[:, colsA].reshape(8, 128, 288).transpose(1, 0, 2))
    return wall, wA


def _params(norm_w, q_norm_w, k_norm_w, sinks, conv_w, conv_b, dt_bias, A_log, D_skip, ssd_norm_w):
    prm = np.zeros((128, NPRM), np.float32)
    p = np.arange(128)
    prm[:, P_NW:P_NW + 8] = norm_w.reshape(8, 128).T
    prm[:, P_WQ] = q_norm_w[p % 64]
    prm[:, P_WK] = k_norm_w[p % 64]
    prm[:, P_CW:P_CW + 96] = conv_w.reshape(4, 24, 128).transpose(2, 1, 0).reshape(128, 96)
    prm[:, P_CB:P_CB + 24] = conv_b.reshape(24, 128).T
    prm[:, P_DTB:P_DTB + 32] = dt_bias[None, :]
    prm[:, P_ALOG:P_ALOG + 32] = A_log[None, :]
    prm[:, P_D:P_D + 16] = D_skip[(np.arange(16)[None, :] * 128 + p[:, None]) // 64]
    prm[:, P_NWS:P_NWS + 16] = ssd_norm_w.reshape(16, 128).T
    prm[:, P_SINK:P_SINK + 16] = sinks[None, :]
    return prm


_CACHE = {}


def kernel(x_prompt, x_sample, cache_k, cache_v, state_conv, state_ssm,
           norm_w, w_in, q_norm_w, k_norm_w, sinks, conv_w, conv_b, dt_bias,
           A_log, D_skip, ssd_norm_w, w_attn_proj, w_ssd_proj, w_out, _do_sample=True):
    f = lambda a: np.ascontiguousarray(np.asarray(a, dtype=np.float32))
    x_prompt, x_sample, cache_k, cache_v, state_conv, state_ssm = map(f, (x_prompt, x_sample, cache_k, cache_v, state_conv, state_ssm))
    wall, wA = _layout_weights(f(w_in)[0], f(w_attn_proj)[0], f(w_ssd_proj)[0], f(w_out)[0])
    prm = _params(f(norm_w)[0], f(q_norm_w)[0], f(k_norm_w)[0], f(sinks)[0], f(conv_w)[0], f(conv_b)[0],
                  f(dt_bias)[0], f(A_log)[0], f(D_skip)[0], f(ssd_norm_w)[0])
    cb, cf, rope = _consts()
    if 'nc' not in _CACHE:
        _CACHE['nc'] = build_program(_do_sample)
    nc = _CACHE['nc']
    in_maps = []
    for c in range(NCORES):
        bs = slice(c * NB, (c + 1) * NB)
        in_maps.append({
            "xp": x_prompt[c * NSEQ:(c + 1) * NSEQ].reshape(NSEQ * SEQ, D),
            "xs": x_sample[bs].reshape(NB * TS, D),
            "wall": wall, "wA": wA, "prm": prm, "cb": cb, "cf": cf, "rope": rope,
            "ck": cache_k[0, bs].reshape(NB, 128, 256),
            "cv": cache_v[0, bs].reshape(NB, 128, 256),
            "ckT": np.ascontiguousarray(cache_k[0, bs].reshape(NB, 128, 256).transpose(0, 2, 1)),
            "sconvT": np.ascontiguousarray(state_conv[0, bs].reshape(NB * 3, 3072).T),
            "sssm": state_ssm[0, bs].reshape(NB, 2048, 128),
            "sssmT": np.ascontiguousarray(state_ssm[0, bs].reshape(NB, 2048, 128).transpose(0, 2, 1)),
        })
    res = run_bass_kernel_spmd(nc, in_maps, core_ids=list(range(NCORES)))
    R = res.results
    cat = lambda k: np.concatenate([np.asarray(r[k], dtype=np.float32) for r in R], 0)
    y_p = cat("yp").reshape(16, SEQ, D)
    y_s = cat("ys").reshape(128, TS, D)
    kwp = cat("kwp").reshape(1, 16, 128, 4, 64)
    vwp = cat("vwp").reshape(1, 16, 128, 4, 64)
    cvp = cat("cvp").reshape(1, 16, 3, 3072)
    ssp = cat("ssp").reshape(1, 16, 32, 64, 128)
    kws = cat("kws").reshape(1, 128, 128, 4, 64)
    vws = cat("vws").reshape(1, 128, 128, 4, 64)
    cvs = cat("cvs").reshape(1, 128, 3, 3072)
    sss = cat("sss").reshape(1, 128, 32, 64, 128)
    return (y_p, y_s, kwp, vwp, cvp, ssp, kws, vws, cvs, sss)
```

```python
import contextlib
import math
import os
STAGE = int(os.environ.get('KSTAGE', '99'))
SUB = int(os.environ.get('KSUB', '99'))
KQ = int(os.environ.get('KQ', '99'))
import numpy as np
import concourse.bass as bass
import concourse.mybir as mybir
from concourse.bass_utils import run_bass_kernel_spmd

F32 = mybir.dt.float32
BF16 = mybir.dt.bfloat16
ALU = mybir.AluOpType
AF = mybir.ActivationFunctionType

NCORES = 8
D = 1024
SEQ = 2048
NSEQ = 2
NB = 16
TS = 8
PAST = 16384
EPS = 1e-6
NT = 256
NG = 27
NDS = 24

G_Q, G_K, G_ZA, G_ZS, G_XBC, G_GA, G_GS, G_PA, G_PS, G_WO = 0, 2, 3, 5, 9, 15, 17, 19, 21, 25

OFF_Q, OFF_K, OFF_V, OFF_ZA, OFF_ZS, OFF_XBC, OFF_DT, OFF_G = 0, 1024, 1280, 1536, 2560, 4608, 7680, 7712

P_NW, P_WQ, P_WK, P_CW, P_CB, P_DTB, P_ALOG, P_D, P_NWS, P_SINK = 0, 8, 9, 10, 106, 130, 162, 194, 210, 226
NPRM = 242
C_ID, C_ONES, C_ONESE, C_ONESO, C_BLK, C_RT, C_M4, C_U, C_L, C_UB, C_LB, C_SEL, C_MSC, C_MSN = (
    0, 128, 256, 384, 512, 640, 768, 1280, 1408, 1536, 1664, 1792, 1808, 1816)
C_E0 = 1816 + 128
NCB = C_E0 + 128
F_ID, F_COSS, F_SINS, F_SEL = 0, 128, 256, 384
NCF = 400


class KB:
    def __init__(self, nc, es):
        self.nc = nc
        self.eng = {'pe': nc.tensor, 'act': nc.scalar, 'dve': nc.vector, 'pool': nc.gpsimd, 'sp': nc.sync}
        self.semobj = {}
        for e in self.eng:
            self.semobj[e] = es.enter_context(nc.semaphore("s_" + e))
        for i in range(NDS):
            self.semobj["d%d" % i] = es.enter_context(nc.semaphore("sd%d" % i))
        self.cnt = {e: 0 for e in self.eng}
        self.known = {e: {} for e in self.eng}
        self.state = {}
        self.dcnt = [0] * NDS
        self.dq = {}
        self.pb = 0
        self.nwait = 0
        self.ninst = 0

    def _wait(self, e, k, v):
        if self.known[e].get(k, 0) >= v:
            return
        self.eng[e].wait_ge(self.semobj[k], v)
        self.known[e][k] = v
        self.nwait += 1

    def _need(self, e, reads, writes):
        need = {}

        def add(k, v):
            if need.get(k, 0) < v:
                need[k] = v
        for key in reads:
            st = self.state.get(key)
            if st and st[0]:
                add(*st[0])
            if st and isinstance(key, tuple) and key[0] == 'ps':
                for k, v in st[1].items():
                    if k != e:
                        add(k, v)
        for key in writes:
            st = self.state.get(key)
            if st:
                if st[0]:
                    add(*st[0])
                for k, v in st[1].items():
                    add(k, v)
        for k, v in need.items():
            if k == e and e == 'pe':
                continue
            self._wait(e, k, v)

    def _commit(self, ev, reads, writes):
        for key in writes:
            self.state[key] = [ev, {}]
        for key in reads:
            st = self.state.setdefault(key, [None, {}])
            if st[1].get(ev[0], 0) < ev[1]:
                st[1][ev[0]] = ev[1]

    def op(self, e, fn, reads=(), writes=(), inc=True):
        self._need(e, reads, writes)
        inst = fn(self.eng[e])
        self.ninst += 1
        if inc:
            self.cnt[e] += 1
            inst.then_inc(self.semobj[e], 1)
            ev = (e, self.cnt[e])
        else:
            ev = (e, self.cnt[e] + 1)
        self._commit(ev, reads, writes)

    def dma(self, q, out, in_, reads=(), writes=()):
        lo, hi = (0, 8) if q == 'pool' else (8, NDS)
        i = self.dq.get(q, lo)
        self.dq[q] = i + 1 if i + 1 < hi else lo
        k = "d%d" % i
        if self.dcnt[i]:
            self._wait(q, k, self.dcnt[i])
        self._need(q, reads, writes)
        self.dcnt[i] += 16
        self.eng[q].dma_start(out=out, in_=in_).then_inc(self.semobj[k], 16)
        self.ninst += 1
        self._commit((k, self.dcnt[i]), reads, writes)

    def psum(self):
        i = self.pb
        self.pb = (self.pb + 1) % 8
        return i

    def finish(self):
        for i in range(NDS):
            if self.dcnt[i]:
                self._wait('sp', "d%d" % i, self.dcnt[i])
        for e in ('pe', 'act', 'dve', 'pool'):
            if self.cnt[e]:
                self._wait('sp', e, self.cnt[e])


def build_program(do_sample=True, max_tiles=None):
    nc = bass.Bass("TRN2", target_bir_lowering=False)

    def din(name, shape):
        return nc.dram_tensor(name, list(shape), F32, kind="ExternalInput").ap()

    def dout(name, shape):
        return nc.dram_tensor(name, list(shape), F32, kind="ExternalOutput").ap()

    xp = din("xp", [NSEQ * SEQ, D])
    xs = din("xs", [NB * TS, D])
    wall = din("wall", [NG, 128, 8, 512])
    wA = din("wA", [128, 8, 288])
    prm_d = din("prm", [128, NPRM])
    cb_d = din("cb", [128, NCB])
    cf_d = din("cf", [128, NCF])
    rope_d = din("rope", [2, 128, SEQ])
    ck_d = din("ck", [NB, 128, 256])
    cv_d = din("cv", [NB, 128, 256])
    ckT_d = din("ckT", [NB, 256, 128])
    scT_d = din("sconvT", [3072, NB * 3])
    ss_d = din("sssm", [NB, 2048, 128])
    ssT_d = din("sssmT", [NB, 128, 2048])

    yp = dout("yp", [NSEQ * SEQ, D])
    ys = dout("ys", [NB * TS, D])
    kwp = dout("kwp", [NSEQ, 128, 256])
    vwp = dout("vwp", [NSEQ, 128, 256])
    cvp = dout("cvp", [NSEQ, 3, 3072])
    ssp = dout("ssp", [NSEQ, 2048, 128])
    kws = dout("kws", [NB, 128, 256])
    vws = dout("vws", [NB, 128, 256])
    cvs = dout("cvs", [NB, 3, 3072])
    sss = dout("sss", [NB, 2048, 128])
    scr = nc.dram_tensor("scr", [128, 3072], F32, kind="Internal").ap()
    wbf = nc.dram_tensor("wbf", [NG, 128, 8, 512], BF16, kind="Internal").ap()

    es = contextlib.ExitStack()
    with es:
        kb = KB(nc, es)

        def sb(name, shape, dt=BF16):
            return es.enter_context(nc.sbuf_tensor(name, list(shape), dt))

        banks = [es.enter_context(nc.psum_tensor("bank%d" % i, [128, 512], F32)) for i in range(8)]

        def bank():
            i = kb.psum()
            return banks[i], ('ps', i)

        wbuf = [sb("wbuf%d" % i, [128, 8, 512]) for i in range(3)]
        wAt = sb("wAt", [128, 8, 288])
        prm = sb("prm_sb", [128, NPRM], F32)
        cb = sb("cb_sb", [128, NCB])
        cf = sb("cf_sb", [128, NCF], F32)
        rope = sb("rope_sb", [128, 2, NT], F32)
        xin = [sb("xin%d" % i, [128, D], F32) for i in range(2)]
        xn = sb("xn", [128, D])
        hT = sb("hT", [128, 8, NT])
        qT = sb("qT", [128, 8, NT])
        mT = qT
        kTE = sb("kTE", [128, 4, 128 + NT])
        kTO = sb("kTO", [128, 4, 128 + NT])
        NCH = NT // 128
        VE = sb("VE", [128, 1 + NCH, 4, 128])
        VO = sb("VO", [128, 1 + NCH, 4, 128])
        zaT = sb("zaT", [128, 8, NT])
        ogT = zaT
        zsT = sb("zsT", [128, 16, NT])
        xbcT = sb("xbcT", [128, 24, NT])
        gT = sb("gT", [128, 16, NT])
        ynT = sb("ynT", [128, 16, NT])
        qraw = [sb("qraw%d" % i, [128, NT]) for i in range(2)]
        qsq = [sb("qsq%d" % i, [128, NT]) for i in range(2)]
        t1 = [sb("t1_%d" % i, [128, NT], F32) for i in range(2)]
        t2 = [sb("t2_%d" % i, [128, NT], F32) for i in range(2)]
        rsq = [sb("rsq%d" % i, [128, NT], F32) for i in range(2)]
        kf32 = sb("kf32", [128, 4, 128], F32)
        xraw = [sb("xraw%d" % i, [128, 3 + NT], F32) for i in range(4)]
        acc = [sb("acc%d" % i, [128, NT], F32) for i in range(4)]
        hist = sb("hist", [128, 24, 3], F32)
        Pt = [sb("Pt%d" % i, [128, 512]) for i in range(6)]
        Rr = [sb("Rr%d" % i, [128, 128], F32) for i in range(3)]
        ogf = [sb("ogf%d" % i, [128, 128], F32) for i in range(3)]
        sinkbc = sb("sinkbc", [128, 8, 128])
        esink = sb("esink", [128, 16], F32)
        RTq = sb("RTq", [128, 128])
        RTk = sb("RTk", [128, 128])
        Aneg = sb("Aneg", [128, 32], F32)
        sm = sb("sm", [128, 8], F32)
        dtt3 = sb("dtt3", [128, NT // 128, 32], F32)
        av3 = sb("av3", [128, NT // 128, 32], F32)
        ahi3 = sb("ahi3", [128, NT // 128, 32])
        alo3 = sb("alo3", [128, NT // 128, 32])
        atmp3 = sb("atmp3", [128, NT // 128, 32], F32)
        dte3 = sb("dte3", [128, NT // 128, 32], F32)
        cdb3 = sb("cdb3", [128, NT // 128, 32], F32)
        abcH = sb("abcH", [128, 32, 64])
        abcL = sb("abcL", [128, 32, 64])
        dxE = sb("dxE", [128, 16, 128])
        dxO = sb("dxO", [128, 16, 128])
        dxe = sb("dxe", [128, 2048])
        Btok = sb("Btok", [128, 512])
        AUh = [sb("AUh%d" % i, [128, 8, 128]) for i in range(2)]
        AUl = [sb("AUl%d" % i, [128, 8, 128]) for i in range(2)]
        Eb = [sb("Eb%d" % i, [128, 8, 128]) for i in range(2)]
        cbm = [sb("cbm%d" % i, [128, 128]) for i in range(2)]
        ea = [sb("ea%d" % i, [128, 128], F32) for i in range(4)]
        yt = [sb("yt%d" % i, [128, 128], F32) for i in range(4)]
        yb = sb("yb", [128, 16, 128])
        ysq = sb("ysq", [128, 16, 128])
        rsn = sb("rsn", [128, 4, 128], F32)
        S = sb("S", [128, 2048], F32)
        Sb = sb("Sb", [128, 2048])
        sttr = [sb("sttr%d" % i, [128, 512], F32) for i in range(3)]
        rr = {'se': 0, 'q': 0, 'x': 0, 'p': 0, 'e': 0, 'y': 0, 'yb': 0, 'm1': 0, 'st': 0, 'xin': 0}

        def rot(name, n=2):
            i = rr[name]
            rr[name] = (i + 1) % n
            return i

        print("sbuf bytes remaining/partition:", nc.sbuf_bytes_remaining)

        kb.dma('sp', prm[:], prm_d, writes=['prm'])
        kb.dma('sp', cf[:], cf_d, writes=['cf'])
        kb.dma('pool', cb[:], cb_d, writes=['cb'])
        kb.dma('pool', wAt[:], wA, writes=['wAt'])
        ident = cb[:, C_ID:C_ID + 128]
        ones = cb[:, C_ONES:C_ONES + 128]
        onesE = cb[:, C_ONESE:C_ONESE + 128]
        onesO = cb[:, C_ONESO:C_ONESO + 128]
        blk = cb[:, C_BLK:C_BLK + 128]
        identf = cf[:, F_ID:F_ID + 128]
        kb.op('dve', lambda e: e.tensor_scalar(out=RTq[:], in0=cb[:, C_RT:C_RT + 128], scalar1=prm[:, P_WQ:P_WQ + 1],
                                               scalar2=None, op0=ALU.mult), reads=['cb', 'prm'], writes=['RTq'])
        kb.op('dve', lambda e: e.tensor_scalar(out=RTk[:], in0=cb[:, C_RT:C_RT + 128], scalar1=prm[:, P_WK:P_WK + 1],
                                               scalar2=None, op0=ALU.mult), reads=['cb', 'prm'], writes=['RTk'])
        kb.op('act', lambda e: e.activation(out=Aneg[:], in_=prm[:, P_ALOG:P_ALOG + 32], func=AF.Exp),
              reads=['prm'], writes=['Aneg'])
        kb.op('dve', lambda e: e.tensor_scalar(out=Aneg[:], in0=Aneg[:], scalar1=-1.0, scalar2=None, op0=ALU.mult),
              reads=['Aneg'], writes=['Aneg'])
        kb.op('act', lambda e: e.activation(out=esink[:], in_=prm[:, P_SINK:P_SINK + 16], func=AF.Exp),
              reads=['prm'], writes=['esink'])
        kb.op('dve', lambda e: e.tensor_copy(
            out=sinkbc[:, :, :].rearrange("p i (e d) -> p i e d", e=2),
            in_=esink[:, :].rearrange("p (i e) -> p i e", e=2).unsqueeze(3).to_broadcast([128, 8, 2, 64])),
            reads=['esink'], writes=['sinkbc'])
        kb.op('pool', lambda e: e.memset(kTE[:], 0.0), writes=[('kT', g_) for g_ in range(4)] + ['kTprev'])
        kb.op('pool', lambda e: e.memset(kTO[:], 0.0), writes=[('kT', g_) for g_ in range(4)] + ['kTprev'])
        for t_, nm in ((VE, 'VE'), (VO, 'VO'), (dxE, 'dxE'), (dxO, 'dxO')):
            kb.op('pool', lambda e, t_=t_: e.memset(t_[:], 0.0), writes=[nm] if nm[0] == 'd' else [('V', s_) for s_ in range(NCH + 1)])
        kb.op('pool', lambda e: e.memset(S[:], 0.0), writes=['S'] + [('S', g_) for g_ in range(4)])
        kb.op('pool', lambda e: e.memset(Sb[:], 0.0), writes=['Sb'] + [('Sb', g_) for g_ in range(4)])
        kb.op('pool', lambda e: e.memset(hist[:], 0.0), writes=[('hist', ci) for ci in range(24)])

        wstate = {'issued': 0, 'used': 0}
        WORDER = list(range(5, 15)) + list(range(0, 5)) + list(range(15, NG))

        def load_w(g):
            k = wstate['used']
            assert WORDER[k % NG] == g, (k, g)
            wstate['used'] = k + 1
            while wstate['issued'] <= k + 2 and wstate['issued'] < wstate['total']:
                j = wstate['issued']
                gj = WORDER[j % NG]
                if j < NG:
                    kb.dma('pool', wbf[gj], wall[gj], writes=[('wbf', gj)])
                kb.dma('pool', wbuf[j % 3][:], wbf[gj], reads=[('wbf', gj)], writes=[('w', j % 3)])
                wstate['issued'] = j + 1
            return k % 3

        def rmsnorm_to_hT(xsrc_rows, ntok_chunks):
            for c in range(ntok_chunks):
                xi = rot('xin')
                kb.dma('sp', xin[xi][:], xsrc_rows[c * 128:(c + 1) * 128, :], writes=[('xin', xi)])
                kb.op('dve', lambda e: e.memset(sm[:, 0:1], 0.0), writes=['sm'])
                kb.op('act', lambda e: e.activation(out=xn[:], in_=xin[xi][:], func=AF.Square, accum_out=sm[:, 0:1]),
                      reads=[('xin', xi), 'sm'], writes=['xn', 'sm'])
                kb.op('act', lambda e: e.activation(out=sm[:, 1:2], in_=sm[:, 0:1], func=AF.Ln, bias=EPS, scale=1.0 / D),
                      reads=['sm'], writes=['sm'])
                kb.op('act', lambda e: e.activation(out=sm[:, 2:3], in_=sm[:, 1:2], func=AF.Exp, scale=-0.5), reads=['sm'], writes=['sm'])
                kb.op('dve', lambda e: e.tensor_scalar(out=xn[:], in0=xin[xi][:], scalar1=sm[:, 2:3], scalar2=None,
                                                       op0=ALU.mult), reads=[('xin', xi), 'sm'], writes=['xn'])
                bk, bkey = bank()
                bkb = bk[:].bitcast(BF16)
                for kc in range(8):
                    kb.op('pe', lambda e, kc=kc: e.transpose(bkb[:, kc * 128:(kc + 1) * 128], xn[:, kc * 128:(kc + 1) * 128], ident),
                          reads=['xn', 'cb'], writes=[bkey], inc=(kc == 7))
                kb.op('dve', lambda e: e.tensor_tensor(
                    out=hT[:, :, c * 128:(c + 1) * 128], in0=bkb.rearrange("p (k t) -> p k t", k=8),
                    in1=prm[:, P_NW:P_NW + 8].unsqueeze(2).to_broadcast([128, 8, 128]), op=ALU.mult),
                    reads=[bkey, 'prm'], writes=[('hT', c)])

        def mm_group(bk, bkey, wi, cj, n, rhs_tile, rhs_keys, kcs=8, w2=None):
            tot = kcs
            for kc in range(kcs):
                wt = wbuf[wi] if kc < 8 else wbuf[w2]
                wk_ = ('w', wi) if kc < 8 else ('w', w2)
                kb.op('pe', lambda e, kc=kc, wt=wt: e.matmul(bk[:, 0:n], lhsT=wt[:, kc % 8, cj * 128:(cj + 1) * 128],
                                                             rhs=rhs_tile[:, kc, 0:n], start=(kc == 0), stop=(kc == tot - 1)),
                      reads=[wk_] + rhs_keys, writes=[bkey], inc=(kc == tot - 1))

        def drain(gen):
            for _ in gen:
                pass

        def interleave(a, b):
            da = db = False
            while not (da and db):
                if not da:
                    try:
                        next(a)
                    except StopIteration:
                        da = True
                if not db:
                    try:
                        next(b)
                    except StopIteration:
                        db = True

        def chain(*gens):
            for g_ in gens:
                yield from g_

        def zipg(*gens):
            gens = list(gens)
            while gens:
                for g_ in list(gens):
                    try:
                        next(g_)
                        yield
                    except StopIteration:
                        gens.remove(g_)

        def run_tile(xrows, n, sample, seq, t0, yrows):
            nch = n // 128
            hkeys = [('hT', c) for c in range(nch)]
            first = (not sample) and t0 == 0
            last = (not sample) and (t0 + n == SEQ)
            if not sample:
                kb.dma('sp', rope[:, :, 0:n], rope_d[:, :, t0:t0 + n].rearrange("a p t -> p a t"), writes=['rope'])
                cosT = rope[:, 0, 0:n]
                sinT = rope[:, 1, 0:n]
                ropek = ['rope']
            else:
                cosT = cf[:, F_COSS:F_COSS + 128]
                sinT = cf[:, F_SINS:F_SINS + 128]
                ropek = ['cf']
            if STAGE < 1:
                return
            if sample:
                kb.dma('sp', hists, scT_d.rearrange("(c p) x -> p c x", p=128), writes=['S'] + [('S', g_) for g_ in range(4)])
            rmsnorm_to_hT(xrows, nch)
            if STAGE < 2:
                return

            for c in range(nch):
                bk, bkey = bank()
                for kc in range(8):
                    kb.op('pe', lambda e, kc=kc: e.matmul(bk[:, 0:288], lhsT=hT[:, kc, c * 128:(c + 1) * 128], rhs=wAt[:, kc, :],
                                                          start=(kc == 0), stop=(kc == 7)),
                          reads=[('hT', c), 'wAt'], writes=[bkey], inc=(kc == 7))
                vsrc = bk[:, 0:256].rearrange("p (g d) -> p g d", g=4)
                if SUB < 2:
                    continue
                kb.op('act', lambda e: e.activation(out=VE[:, 1 + c, :, 0:64], in_=vsrc, func=AF.Copy), reads=[bkey], writes=[('V', 1 + c)])
                if SUB < 3:
                    continue
                kb.op('act', lambda e: e.activation(out=VO[:, 1 + c, :, 64:128], in_=vsrc, func=AF.Copy), reads=[bkey], writes=[('V', 1 + c)])
                if SUB < 4:
                    continue
                kb.op('dve', lambda e: e.tensor_tensor(out=dtall[:, c, :], in0=bk[:, 256:288], in1=prm[:, P_DTB:P_DTB + 32], op=ALU.add),
                      reads=[bkey, 'prm'], writes=[('dtall', c)])
                ssd_prep(c, sample)
                yield
                if (last and c == nch - 1) or sample:
                    si = rot('st', 3)
                    kb.op('act', lambda e: e.activation(out=sttr[si][:, 0:256], in_=bk[:, 0:256], func=AF.Copy), reads=[bkey], writes=[('sttr', si)])
                    if not sample:
                        kb.dma('sp', vwp[seq], sttr[si][:, 0:256], reads=[('sttr', si)])
                    else:
                        for b in range(NB):
                            kb.dma('sp', vws[b, 120:128, :], sttr[si][b * 8:(b + 1) * 8, 0:256], reads=[('sttr', si)])

            yield 'front_done'
            def qk_chunk(bk, bkey, is_q, ci):
                i = rot('q')
                w_col = prm[:, P_WQ:P_WQ + 1] if is_q else prm[:, P_WK:P_WK + 1]
                RT = RTq if is_q else RTk
                RTn = 'RTq' if is_q else 'RTk'
                kb.op('act', lambda e: e.activation(out=qraw[i][:, 0:n], in_=bk[:, 0:n], func=AF.Copy), reads=[bkey], writes=[('qraw', i)])
                kb.op('act', lambda e: e.activation(out=qsq[i][:, 0:n], in_=bk[:, 0:n], func=AF.Square), reads=[bkey], writes=[('qsq', i)])
                kb.op('dve', lambda e: e.scalar_tensor_tensor(out=t1[i][:, 0:n], in0=bk[:, 0:n], scalar=w_col, in1=cosT,
                                                              op0=ALU.mult, op1=ALU.mult),
                      reads=[bkey, 'prm'] + ropek, writes=[('t1', i)])
                return qk_stages(is_q, ci, i, RT, RTn)

            def qk_stages(is_q, ci, i, RT, RTn):
                st8 = {}

                def s_ss():
                    st8['b2'] = bank()
                    b2, b2key = st8['b2']
                    kb.op('pe', lambda e: e.matmul(b2[:, 0:n], lhsT=blk, rhs=qsq[i][:, 0:n], start=True, stop=True),
                          reads=['cb', ('qsq', i)], writes=[b2key])

                def s_rot():
                    st8['b3'] = bank()
                    b3, b3key = st8['b3']
                    kb.op('pe', lambda e: e.matmul(b3[:, 0:n], lhsT=RT[:], rhs=qraw[i][:, 0:n], start=True, stop=True),
                          reads=[RTn, ('qraw', i)], writes=[b3key])

                def s_ln():
                    b2, b2key = st8['b2']
                    kb.op('act', lambda e: e.activation(out=rsq[i][:, 0:n], in_=b2[:, 0:n], func=AF.Ln, bias=EPS, scale=1.0 / 64),
                          reads=[b2key], writes=[('rsq', i)])

                def s_t2():
                    b3, b3key = st8['b3']
                    kb.op('dve', lambda e: e.tensor_tensor(out=t2[i][:, 0:n], in0=b3[:, 0:n], in1=sinT, op=ALU.mult),
                          reads=[b3key] + ropek, writes=[('t2', i)])

                def s_exp():
                    kb.op('act', lambda e: e.activation(out=rsq[i][:, 0:n], in_=rsq[i][:, 0:n], func=AF.Exp, scale=-0.5), reads=[('rsq', i)], writes=[('rsq', i)])

                def s_add():
                    kb.op('dve', lambda e: e.tensor_tensor(out=t1[i][:, 0:n], in0=t2[i][:, 0:n], in1=t1[i][:, 0:n], op=ALU.add),
                          reads=[('t2', i), ('t1', i)], writes=[('t1', i)])

                def s_fin():
                    if is_q:
                        kb.op('dve', lambda e: e.tensor_tensor(out=qT[:, ci, 0:n], in0=t1[i][:, 0:n], in1=rsq[i][:, 0:n], op=ALU.mult),
                              reads=[('t1', i), ('rsq', i)], writes=[('qT', ci)])
                    else:
                        kb.op('dve', lambda e: e.tensor_tensor(out=kTE[0:64, ci, 128:128 + n], in0=t1[i][0:64, 0:n], in1=rsq[i][0:64, 0:n], op=ALU.mult),
                              reads=[('t1', i), ('rsq', i)], writes=[('kT', ci)])
                        kb.op('dve', lambda e: e.tensor_tensor(out=kTO[64:128, ci, 128:128 + n], in0=t1[i][64:128, 0:n], in1=rsq[i][64:128, 0:n], op=ALU.mult),
                              reads=[('t1', i), ('rsq', i)], writes=[('kT', ci)])
                        if last or sample:
                            kb.op('dve', lambda e: e.tensor_tensor(out=kf32[:, ci, :], in0=t1[i][:, n - 128:n], in1=rsq[i][:, n - 128:n], op=ALU.mult),
                                  reads=[('t1', i), ('rsq', i)], writes=[('kf32', ci)])
                return [s_ss, s_rot, s_ln, s_t2, s_exp, s_add, s_fin]

            def xbc_chunk(bk, bkey, ci):
                i = rot('x', 4)
                nb_, T_ = (1, n) if not sample else (NB, TS)
                W_ = 3 + T_
                xr = xraw[i][:, 0:nb_ * W_].rearrange("p (b w) -> p b w", b=nb_)
                kb.op('act', lambda e: e.activation(out=xr[:, :, 3:W_], in_=bk[:, 0:n].rearrange("p (b t) -> p b t", b=nb_), func=AF.Copy),
                      reads=[bkey], writes=[('xrawb', i)])
                a3 = acc[i][:, 0:n].rearrange("p (b t) -> p b t", b=nb_)
                cw = lambda j: prm[:, P_CW + ci * 4 + j:P_CW + ci * 4 + j + 1]
                xk = [('xrawb', i), ('xrawh', i)]
                st = []
                st.append(lambda: kb.op('act', lambda e: e.activation(
                    out=xr[:, :, 0:3], in_=hist[:, ci, 0:nb_ * 3].rearrange("p (b j) -> p b j", b=nb_) if not sample
                    else hists[:, ci, :].rearrange("p (b j) -> p b j", b=nb_), func=AF.Copy),
                    reads=[('hist', ci)] if not sample else ['S'], writes=[('xrawh', i)]))
                st.append(lambda: kb.op('act', lambda e: e.activation(out=a3, in_=xr[:, :, 0:T_], func=AF.Copy, scale=cw(0)),
                                        reads=xk + ['prm'], writes=[('acc', i)]))
                if not sample:
                    st.append(lambda: kb.op('act', lambda e: e.activation(out=hist[:, ci, :], in_=xraw[i][:, n:n + 3], func=AF.Copy),
                                            reads=[('xrawb', i)], writes=[('hist', ci)]))
                for j in (1, 2, 3):
                    st.append(lambda j=j: kb.op('dve', lambda e: e.scalar_tensor_tensor(out=a3, in0=xr[:, :, j:j + T_], scalar=cw(j), in1=a3,
                                                                                        op0=ALU.mult, op1=ALU.add),
                                                reads=xk + ['prm', ('acc', i)], writes=[('acc', i)]))
                st.append(lambda: kb.op('act', lambda e: e.activation(out=xbcT[:, ci, 0:n], in_=acc[i][:, 0:n], func=AF.Silu,
                                                                      bias=prm[:, P_CB + ci:P_CB + ci + 1], scale=1.0),
                                        reads=[('acc', i), 'prm'], writes=[('xbc', ci)]))
                return st

            def lockstep(stage_lists):
                for k in range(max(len(l_) for l_ in stage_lists)):
                    for l_ in stage_lists:
                        if k < len(l_):
                            l_[k]()

            def stream(g0, ng, handler):
                pend = []
                for g in range(g0, g0 + ng):
                    wi = load_w(g)
                    for cj in range(4):
                        bk, bkey = bank()
                        mm_group(bk, bkey, wi, cj, n, hT, hkeys)
                        if len(pend) >= 2:
                            lockstep(pend)
                            del pend[:]
                        r_ = handler(bk, bkey, (g - g0) * 4 + cj, wi)
                        if r_:
                            pend.append(r_)
                        yield
                if pend:
                    lockstep(pend)
                    yield

            xpend = []

            def xbc_handler(bk, bkey, ci, wi):
                xpend.append(xbc_chunk(bk, bkey, ci))
                if ci % 4 == 3:
                    lockstep(xpend)
                    del xpend[:]
                if (last or sample) and ci % 4 == 3:
                    g4 = ci // 4
                    b2, b2key = bank()
                    for kc in range(8):
                        kb.op('pe', lambda e, kc=kc: e.matmul(b2[:, 0:512], lhsT=hT[:, kc, n - 128:n], rhs=wbuf[wi][:, kc, :],
                                                              start=(kc == 0), stop=(kc == 7)),
                              reads=[('hT', nch - 1), ('w', wi)], writes=[b2key], inc=(kc == 7))
                    si = rot('st', 3)
                    kb.op('act', lambda e: e.activation(out=sttr[si][:], in_=b2[:, 0:512], func=AF.Copy),
                          reads=[b2key], writes=[('sttr', si)])
                    if not sample:
                        kb.dma('sp', cvp[seq, :, g4 * 512:(g4 + 1) * 512], sttr[si][125:128, :], reads=[('sttr', si)])
                    else:
                        kb.dma('sp', scr[:, g4 * 512:(g4 + 1) * 512], sttr[si][:], reads=[('sttr', si)], writes=['scr'])

            def ssd_streams():
                yield from stream(G_ZS, 4, lambda bk, bkey, ci, wi: kb.op(
                    'act', lambda e: e.activation(out=zsT[:, ci, 0:n], in_=bk[:, 0:n], func=AF.Silu), reads=[bkey], writes=[('zs', ci)]))
                yield from stream(G_XBC, 6, xbc_handler)

            def attn_all():
                yield from attention_tile(nch, n, first)
                kb.op('pool', lambda e: e.tensor_copy(out=kTE[:, :, 0:128], in_=kTE[:, :, n:n + 128]),
                      reads=[('kT', g) for g in range(4)], writes=['kTprev'])
                kb.op('pool', lambda e: e.tensor_copy(out=kTO[:, :, 0:128], in_=kTO[:, :, n:n + 128]),
                      reads=[('kT', g) for g in range(4)], writes=['kTprev'])
                kb.op('pool', lambda e: e.tensor_copy(out=VE[:, 0, :, :], in_=VE[:, nch, :, :]), reads=[('V', nch)], writes=[('V', 0)])
                kb.op('pool', lambda e: e.tensor_copy(out=VO[:, 0, :, :], in_=VO[:, nch, :, :]), reads=[('V', nch)], writes=[('V', 0)])

            drain(ssd_streams())
            if sample:
                kb.dma('sp', cvs, scr.rearrange("(b t) c -> b t c", t=TS)[:, 5:8, :], reads=['scr'])

            def qk_za_attention():
                yield from stream(G_Q, 2, lambda bk, bkey, ci, wi: qk_chunk(bk, bkey, True, ci))
                yield from stream(G_K, 1, lambda bk, bkey, ci, wi: qk_chunk(bk, bkey, False, ci))
                yield from stream(G_ZA, 2, lambda bk, bkey, ci, wi: kb.op(
                    'act', lambda e: e.activation(out=zaT[:, ci, 0:n], in_=bk[:, 0:n], func=AF.Copy), reads=[bkey], writes=[('za', ci)]))
                kb.op('act', lambda e: e.activation(out=zaT[:, :, 0:n], in_=zaT[:, :, 0:n], func=AF.Silu),
                      reads=[('za', ci) for ci in range(8)], writes=[('za', ci) for ci in range(8)])
                if last or sample:
                    si = rot('st', 3)
                    for g in range(4):
                        bk, bkey = bank()
                        kb.op('pe', lambda e: e.transpose(bk[:, 0:128], kf32[:, g, :], identf),
                              reads=[('kf32', g), 'cf'], writes=[bkey])
                        kb.op('act', lambda e: e.activation(out=sttr[si][:, g * 64:(g + 1) * 64], in_=bk[:, 0:64], func=AF.Copy),
                              reads=[bkey], writes=[('sttr', si)])
                    if not sample:
                        kb.dma('sp', kwp[seq], sttr[si][:, 0:256], reads=[('sttr', si)])
                    else:
                        for b in range(NB):
                            kb.dma('sp', kws[b, 120:128, :], sttr[si][b * 8:(b + 1) * 8, 0:256], reads=[('sttr', si)])
                yield
                if not sample:
                    yield from zipg(attn_all(), gate_stream())
                else:
                    attention_sample()
                    yield

            if STAGE < 7:
                return
            if sample:
                ssd_sample_pre()
            def ssd_all():
                for c in range(nch):
                    yield from ssd_chunk(c, n, sample, first and c == 0)

            def gate_stream():
                yield from stream(G_GA, 4, lambda bk, bkey, ci, wi: kb.op(
                    'act', lambda e: e.activation(out=gT[:, ci, 0:n], in_=bk[:, 0:n], func=AF.Copy), reads=[bkey], writes=[('g', ci)]))

            def gate_sigmoid():
                kb.op('act', lambda e: e.activation(out=gT[:, :, 0:n], in_=gT[:, :, 0:n], func=AF.Sigmoid),
                      reads=[('g', ci) for ci in range(16)], writes=[('g', ci) for ci in range(16)])

            def attn_proj():
                ogk_ = [('za', i_) for i_ in range(8)]
                for g in range(2):
                    wi = load_w(G_PA + g)
                    for cj in range(4):
                        co = g * 4 + cj
                        bk, bkey = bank()
                        mm_group(bk, bkey, wi, cj, n, ogT, ogk_)
                        kb.op('dve', lambda e: e.tensor_tensor(out=mT[:, co, 0:n], in0=bk[:, 0:n], in1=gT[:, co, 0:n], op=ALU.mult),
                              reads=[bkey, ('g', co)], writes=[('qT', co)])
                        yield

            def mid_b():
                yield from qk_za_attention()
                gate_sigmoid()
                yield from attn_proj()

            if not sample:
                interleave(ssd_all(), mid_b())
            else:
                drain(qk_za_attention())
                drain(ssd_all())
                drain(gate_stream())
            if sample:
                gate_sigmoid()
            if last:
                for j in range(16):
                    bk, bkey = bank()
                    kb.op('pe', lambda e: e.transpose(bk[:, 0:128], S[:, j * 128:(j + 1) * 128], identf), reads=[('S', j // 4), 'cf'], writes=[bkey])
                    si = rot('st', 3)
                    kb.op('act', lambda e: e.activation(out=sttr[si][:, 0:128], in_=bk[:, 0:128], func=AF.Copy), reads=[bkey], writes=[('sttr', si)])
                    kb.dma('sp', ssp[seq, j * 128:(j + 1) * 128, :], sttr[si][:, 0:128], reads=[('sttr', si)])
                kb.op('pool', lambda e: e.memset(S[:], 0.0), writes=['S'] + [('S', g_) for g_ in range(4)])
                kb.op('pool', lambda e: e.memset(Sb[:], 0.0), writes=['Sb'] + [('Sb', g_) for g_ in range(4)])
                kb.op('pool', lambda e: e.memset(hist[:], 0.0), writes=[('hist', ci) for ci in range(24)])

            yield 'body_done'
            ogk = [('za', i_) for i_ in range(8)]
            ynk = [('yn', c, ci_) for c in range(nch) for ci_ in range(16)]
            if sample:
                yield from attn_proj()
            for g in range(2):
                wi = load_w(G_PS + 2 * g)
                bks = [bank() for _ in range(4)]
                for cj in range(4):
                    bk, bkey = bks[cj]
                    for kc in range(8):
                        kb.op('pe', lambda e, kc=kc: e.matmul(bk[:, 0:n], lhsT=wbuf[wi][:, kc, cj * 128:(cj + 1) * 128], rhs=ynT[:, kc, 0:n],
                                                              start=(kc == 0), stop=False), reads=[('w', wi)] + ynk, writes=[bkey], inc=False)
                wi2 = load_w(G_PS + 2 * g + 1)
                for cj in range(4):
                    co = g * 4 + cj
                    bk, bkey = bks[cj]
                    for kc in range(8):
                        kb.op('pe', lambda e, kc=kc: e.matmul(bk[:, 0:n], lhsT=wbuf[wi2][:, kc, cj * 128:(cj + 1) * 128], rhs=ynT[:, 8 + kc, 0:n],
                                                              start=False, stop=(kc == 7)), reads=[('w', wi2)] + ynk, writes=[bkey], inc=(kc == 7))
                    mi = rot('m1')
                    kb.op('dve', lambda e: e.tensor_tensor(out=t2[mi][:, 0:n], in0=bk[:, 0:n], in1=gT[:, 8 + co, 0:n], op=ALU.mult),
                          reads=[bkey, ('g', 8 + co)], writes=[('t2', mi)])
                    kb.op('dve', lambda e: e.tensor_tensor(out=mT[:, co, 0:n], in0=t2[mi][:, 0:n], in1=mT[:, co, 0:n], op=ALU.add),
                          reads=[('t2', mi), ('qT', co)], writes=[('qT', co)])
                yield
            mk = [('qT', co) for co in range(8)]
            ystores = []
            xis = []
            for c in range(nch):
                xi = rot('xin')
                kb.dma('sp', xin[xi][:], xrows[c * 128:(c + 1) * 128, :], writes=[('xin', xi)])
                xis.append(xi)
            for g in range(2):
                wo = load_w(G_WO + g)
                for c in range(nch):
                    xi = xis[c]
                    bk, bkey = bank()
                    for kc in range(8):
                        kb.op('pe', lambda e, kc=kc: e.matmul(bk[:, 0:512], lhsT=mT[:, kc, c * 128:(c + 1) * 128], rhs=wbuf[wo][:, kc, :],
                                                              start=(kc == 0), stop=(kc == 7)),
                              reads=mk + [('w', wo)], writes=[bkey], inc=(kc == 7))
                    yi = rot('st', 3)
                    kb.op('dve', lambda e: e.tensor_tensor(out=sttr[yi][:], in0=bk[:, 0:512], in1=xin[xi][:, g * 512:(g + 1) * 512], op=ALU.add),
                          reads=[bkey, ('xin', xi)], writes=[('sttr', yi)])
                    ystores.append((yrows[c * 128:(c + 1) * 128, g * 512:(g + 1) * 512], sttr[yi][:], ('sttr', yi)))
                    if len(ystores) >= 2:
                        o_, i_, k_ = ystores.pop(0)
                        kb.dma('sp', o_, i_, reads=[k_])
                    yield

            for o_, i_, k_ in ystores:
                kb.dma('sp', o_, i_, reads=[k_])

        def attention_chunk(c, n, noprev):
            q0 = c * 128
            kcur = 128 + q0
            kprev = q0
            terms = [(half, kt) for half in range(2) for kt in range(2) if not (noprev and kt == 0)]
            vkeys = [('V', c), ('V', c + 1)]

            def scores(i):
                g = i // 2
                bk, bkey = bank()
                kkeys = [('kT', g), 'kTprev']
                for ti, (half, kt) in enumerate(terms):
                    kTx = kTE if half == 0 else kTO
                    koff = kprev if kt == 0 else kcur
                    col = (half * 2 + kt) * 128
                    kb.op('pe', lambda e: e.matmul(bk[:, col:col + 128], lhsT=kTx[:, g, koff:koff + 128], rhs=qT[:, i, q0:q0 + 128],
                                                   start=True, stop=True),
                          reads=kkeys + [('qT', i)], writes=[bkey], inc=(ti == len(terms) - 1))
                pi = rot('p', 6)
                if noprev:
                    for half in range(2):
                        col = (half * 2 + 1) * 128
                        kb.op('act', lambda e: e.activation(out=Pt[pi][:, col:col + 128], in_=bk[:, col:col + 128], func=AF.Exp, scale=0.125),
                              reads=[bkey], writes=[('P', pi)])
                        kb.op('dve', lambda e: e.tensor_tensor(out=Pt[pi][:, col:col + 128], in0=Pt[pi][:, col:col + 128],
                                                               in1=cb[:, C_M4 + 128:C_M4 + 256], op=ALU.mult),
                              reads=[('P', pi), 'cb'], writes=[('P', pi)])
                else:
                    kb.op('act', lambda e: e.activation(out=Pt[pi][:], in_=bk[:, 0:512], func=AF.Exp, scale=0.125), reads=[bkey], writes=[('P', pi)])
                    kb.op('dve', lambda e: e.tensor_tensor(out=Pt[pi][:], in0=Pt[pi][:], in1=cb[:, C_M4:C_M4 + 512], op=ALU.mult),
                          reads=[('P', pi), 'cb'], writes=[('P', pi)])
                return pi

            def finish_stages(i, pi):
                g = i // 2
                st8 = {}

                def s_pe():
                    st8['bo'] = bank()
                    bo, bokey = st8['bo']
                    for ti, (half, kt) in enumerate(terms):
                        col = (half * 2 + kt) * 128
                        Vt = VE if half == 0 else VO
                        kb.op('pe', lambda e: e.matmul(bo[:, 0:128], lhsT=Vt[:, c + kt, g, :], rhs=Pt[pi][:, col:col + 128],
                                                       start=(ti == 0), stop=(ti == len(terms) - 1)),
                              reads=vkeys + [('P', pi)], writes=[bokey], inc=False)
                    for ti, (half, kt) in enumerate(terms):
                        col = (half * 2 + kt) * 128
                        on = onesE if half == 0 else onesO
                        kb.op('pe', lambda e: e.matmul(bo[:, 128:256], lhsT=on, rhs=Pt[pi][:, col:col + 128], start=(ti == 0), stop=False),
                              reads=['cb', ('P', pi)], writes=[bokey], inc=False)
                    kb.op('pe', lambda e: e.matmul(bo[:, 128:256], lhsT=sinkbc[:, i, :], rhs=cb[:, C_E0:C_E0 + 128], start=False, stop=True),
                          reads=['sinkbc', 'cb'], writes=[bokey])
                    st8['ri'] = rot('e', 3)

                def s_ln():
                    bo, bokey = st8['bo']
                    ri = st8['ri']
                    kb.op('act', lambda e: e.activation(out=Rr[ri][:], in_=bo[:, 128:256], func=AF.Ln), reads=[bokey], writes=[('Rr', ri)])

                def s_exp():
                    ri = st8['ri']
                    kb.op('act', lambda e: e.activation(out=Rr[ri][:], in_=Rr[ri][:], func=AF.Exp, scale=-1.0), reads=[('Rr', ri)], writes=[('Rr', ri)])

                def s_og():
                    bo, bokey = st8['bo']
                    ri = st8['ri']
                    kb.op('dve', lambda e: e.tensor_tensor(out=ogf[ri][:], in0=bo[:, 0:128], in1=Rr[ri][:], op=ALU.mult),
                          reads=[bokey, ('Rr', ri)], writes=[('ogf', ri)])

                def s_ogz():
                    ri = st8['ri']
                    kb.op('dve', lambda e: e.tensor_tensor(out=ogT[:, i, q0:q0 + 128], in0=ogf[ri][:], in1=zaT[:, i, q0:q0 + 128], op=ALU.mult),
                          reads=[('ogf', ri), ('za', i)], writes=[('za', i)])
                return [s_pe, s_ln, s_exp, s_og, s_ogz]

            return scores, finish_stages

        def attention_tile(nch, n, first):
            chunks = [attention_chunk(c, n, first and c == 0) for c in range(nch)]
            pend = [[] for _ in chunks]
            for i in range(8 + 2):
                if i < 8:
                    for ch, (sc_, _f) in enumerate(chunks):
                        pend[ch].append(sc_(i))
                    yield
                if i >= 2:
                    lockstep_g([f_(i - 2, pend[ch][i - 2]) for ch, (_s, f_) in enumerate(chunks)])
                    yield

        def lockstep_g(stage_lists):
            for k in range(max(len(l_) for l_ in stage_lists)):
                for l_ in stage_lists:
                    if k < len(l_):
                        l_[k]()

        def attention_sample():
            kb.dma('sp', kws[:, 0:120, :], ck_d[:, 8:128, :])
            kb.dma('sp', vws[:, 0:120, :], cv_d[:, 8:128, :])
            kb.op('pool', lambda e: e.memset(Eb[0][:], 0.0), writes=[('Eb', 0)])
            kb.op('pool', lambda e: e.memset(Eb[1][:], 0.0), writes=[('Eb', 1)])
            kb.op('pool', lambda e: e.memset(dxe[:], 0.0), writes=['dxe', 'dxeA', 'dxeB'])
            kkeys_ = [('Eb', 0), 'dxeA']
            vkeys_ = [('Eb', 1), 'dxeB']

            def load_cache(b):
                j = b % 2
                kb.dma('pool', kcE[j][0:64, :, :], ckT_d[b].rearrange("(g d) j -> d g j", g=4), writes=[kkeys_[j]])
                kb.dma('pool', kcO[j][64:128, :, :], ckT_d[b].rearrange("(g d) j -> d g j", g=4), writes=[kkeys_[j]])
                kb.dma('pool', vcE[j][:, :, 0:64], cv_d[b].rearrange("j (g d) -> j g d", g=4), writes=[vkeys_[j]])
                kb.dma('pool', vcO[j][:, :, 64:128], cv_d[b].rearrange("j (g d) -> j g d", g=4), writes=[vkeys_[j]])

            load_cache(0)
            for b in range(NB):
                j = b % 2
                if b + 1 < NB:
                    load_cache(b + 1)
                t0_, t1_ = b * TS, (b + 1) * TS
                bk, bkey = bank()
                for i in range(8):
                    g = i // 2
                    for half in range(2):
                        col = (i * 2 + half) * TS
                        kc_ = kcE[j] if half == 0 else kcO[j]
                        kn_ = kTE if half == 0 else kTO
                        kb.op('pe', lambda e: e.matmul(bk[:, col:col + TS], lhsT=kc_[:, g, :], rhs=qT[:, i, t0_:t1_], start=True, stop=True),
                              reads=[kkeys_[j], ('qT', i)], writes=[bkey], inc=False)
                        kb.op('pe', lambda e: e.matmul(bk[:, 128 + col:128 + col + TS], lhsT=kn_[:, g, 128:256], rhs=qT[:, i, t0_:t1_], start=True, stop=True),
                              reads=[('kT', g), ('qT', i)], writes=[bkey], inc=(i == 7 and half == 1))
                pi = rot('p')
                kb.op('act', lambda e: e.activation(out=Pt[pi][:, 0:256], in_=bk[:, 0:256], func=AF.Exp, scale=0.125), reads=[bkey], writes=[('P', pi)])
                kb.op('dve', lambda e: e.tensor_tensor(
                    out=Pt[pi][:, 0:128].rearrange("p (a t) -> p a t", t=TS), in0=Pt[pi][:, 0:128].rearrange("p (a t) -> p a t", t=TS),
                    in1=cb[:, C_MSC:C_MSC + TS].unsqueeze(1).to_broadcast([128, 16, TS]), op=ALU.mult), reads=[('P', pi), 'cb'], writes=[('P', pi)])
                kb.op('dve', lambda e: e.tensor_tensor(
                    out=Pt[pi][:, 128:256].rearrange("p (a t) -> p a t", t=TS), in0=Pt[pi][:, 128:256].rearrange("p (a t) -> p a t", t=TS),
                    in1=cb[:, C_MSN + t0_:C_MSN + t1_].unsqueeze(1).to_broadcast([128, 16, TS]), op=ALU.mult), reads=[('P', pi), 'cb'], writes=[('P', pi)])
                bo, bokey = bank()
                for i in range(8):
                    g = i // 2
                    ce, co = (i * 2) * TS, (i * 2 + 1) * TS
                    kb.op('pe', lambda e: e.matmul(bo[:, i * TS:(i + 1) * TS], lhsT=vcE[j][:, g, :], rhs=Pt[pi][:, ce:ce + TS], start=True, stop=False),
                          reads=[vkeys_[j], ('P', pi)], writes=[bokey], inc=False)
                    kb.op('pe', lambda e: e.matmul(bo[:, i * TS:(i + 1) * TS], lhsT=vcO[j][:, g, :], rhs=Pt[pi][:, co:co + TS], start=False, stop=False),
                          reads=[vkeys_[j], ('P', pi)], writes=[bokey], inc=False)
                    kb.op('pe', lambda e: e.matmul(bo[:, i * TS:(i + 1) * TS], lhsT=VE[:, 1, g, :], rhs=Pt[pi][:, 128 + ce:128 + ce + TS], start=False, stop=False),
                          reads=[('V', 1), ('P', pi)], writes=[bokey], inc=False)
                    kb.op('pe', lambda e: e.matmul(bo[:, i * TS:(i + 1) * TS], lhsT=VO[:, 1, g, :], rhs=Pt[pi][:, 128 + co:128 + co + TS], start=False, stop=True),
                          reads=[('V', 1), ('P', pi)], writes=[bokey], inc=False)
                kb.op('pe', lambda e: e.matmul(bo[:, 128:256], lhsT=ones, rhs=Pt[pi][:, 0:128], start=True, stop=False), reads=['cb', ('P', pi)], writes=[bokey], inc=False)
                kb.op('pe', lambda e: e.matmul(bo[:, 128:256], lhsT=ones, rhs=Pt[pi][:, 128:256], start=False, stop=False), reads=['cb', ('P', pi)], writes=[bokey], inc=False)
                kb.op('pe', lambda e: e.matmul(bo[:, 128:256], lhsT=cb[:, C_E0:C_E0 + 128], rhs=sinkx[:], start=False, stop=True), reads=['cb', 'sinkx'], writes=[bokey])
                ri = rot('e')
                kb.op('act', lambda e: e.activation(out=Rr[ri][:], in_=bo[:, 128:256], func=AF.Ln), reads=[bokey], writes=[('Rr', ri)])
                kb.op('act', lambda e: e.activation(out=Rr[ri][:], in_=Rr[ri][:], func=AF.Exp, scale=-1.0), reads=[('Rr', ri)], writes=[('Rr', ri)])
                R4 = Rr[ri][:].rearrange("p (i h t) -> p i h t", i=8, h=2)
                O3 = bo[:, 0:64].rearrange("p (i t) -> p i t", i=8)
                og3 = ogf[ri][:, 0:64].rearrange("p (i t) -> p i t", i=8)
                kb.op('dve', lambda e: e.tensor_tensor(out=og3[0:64], in0=O3[0:64], in1=R4[0:64, :, 0, :], op=ALU.mult), reads=[bokey, ('Rr', ri)], writes=[('ogf', ri)])
                kb.op('dve', lambda e: e.tensor_tensor(out=og3[64:128], in0=O3[64:128], in1=R4[64:128, :, 1, :], op=ALU.mult), reads=[bokey, ('Rr', ri)], writes=[('ogf', ri)])
                kb.op('dve', lambda e: e.tensor_tensor(out=ogT[:, :, t0_:t1_], in0=og3, in1=zaT[:, :, t0_:t1_], op=ALU.mult),
                      reads=[('ogf', ri)] + [('za', i) for i in range(8)], writes=[('za', i) for i in range(8)])
            kb.op('pool', lambda e: e.memset(dxe[:, 0:2], 0.0), reads=[], writes=['dxe', 'dxeA', 'dxeB'])

        def ssd_prep(c, sample):
            Lm = cb[:, C_L:C_L + 128] if not sample else cb[:, C_LB:C_LB + 128]
            dtt, av, ahi, alo, atmp, dte, cdb = (t_[:, c, :] for t_ in (dtt3, av3, ahi3, alo3, atmp3, dte3, cdb3))
            K = lambda nm: (nm, c)
            kb.op('act', lambda e: e.activation(out=dtt, in_=dtall[:, c, :], func=AF.Exp), reads=[('dtall', c)], writes=[K('dtt')])
            kb.op('act', lambda e: e.activation(out=dtt, in_=dtt, func=AF.Ln, bias=1.0, scale=1.0), reads=[K('dtt')], writes=[K('dtt')])
            kb.op('dve', lambda e: e.tensor_tensor(out=av, in0=dtt, in1=Aneg[:], op=ALU.mult), reads=[K('dtt'), 'Aneg'], writes=[K('av')])
            kb.op('dve', lambda e: e.tensor_copy(out=ahi, in_=av), reads=[K('av')], writes=[K('ahi')])
            kb.op('dve', lambda e: e.tensor_tensor(out=atmp, in0=av, in1=ahi, op=ALU.subtract), reads=[K('av'), K('ahi')], writes=[K('atmp')])
            kb.op('dve', lambda e: e.tensor_copy(out=alo, in_=atmp), reads=[K('atmp')], writes=[K('alo')])
            bk, bkey = bank()
            kb.op('pe', lambda e: e.matmul(bk[:, 0:32], lhsT=Lm, rhs=ahi, start=True, stop=False), reads=['cb', K('ahi')], writes=[bkey], inc=False)
            kb.op('pe', lambda e: e.matmul(bk[:, 0:32], lhsT=Lm, rhs=alo, start=False, stop=True), reads=['cb', K('alo')], writes=[bkey], inc=False)
            kb.op('pe', lambda e: e.matmul(bk[:, 32:64], lhsT=ones, rhs=ahi, start=True, stop=False), reads=['cb', K('ahi')], writes=[bkey], inc=False)
            kb.op('pe', lambda e: e.matmul(bk[:, 32:64], lhsT=ones, rhs=alo, start=False, stop=True), reads=['cb', K('alo')], writes=[bkey])
            kb.op('act', lambda e: e.activation(out=dte, in_=bk[:, 0:32], func=AF.Exp), reads=[bkey], writes=[K('dte')])
            kb.op('act', lambda e: e.activation(out=cdb, in_=bk[:, 32:64], func=AF.Exp), reads=[bkey], writes=[K('cdb')])
            kb.op('dve', lambda e: e.tensor_tensor(out=dte, in0=dte, in1=dtt, op=ALU.mult), reads=[K('dte'), K('dtt')], writes=[K('dte')])

        def ssd_chunk(c, n, sample, fresh):
            q0 = c * 128
            Um = cb[:, C_U:C_U + 128] if not sample else cb[:, C_UB:C_UB + 128]
            Lm = cb[:, C_L:C_L + 128] if not sample else cb[:, C_LB:C_LB + 128]
            dtt, ahi, alo, dte, cdb = (t_[:, c, :] for t_ in (dtt3, ahi3, alo3, dte3, cdb3))
            kdtt, kahi, kalo, kdte, kcdb = (('dtt', c), ('ahi', c), ('alo', c), ('dte', c), ('cdb', c))
            kb.op('dve', lambda e: e.tensor_copy(out=abcH[:], in_=ahi.unsqueeze(2).to_broadcast([128, 32, 64])), reads=[kahi], writes=['abcH'])
            kb.op('dve', lambda e: e.tensor_copy(out=abcL[:], in_=alo.unsqueeze(2).to_broadcast([128, 32, 64])), reads=[kalo], writes=['abcL'])
            yield
            for half in range(2):
                bt, btkey = bank()
                btb = bt[:].bitcast(BF16)
                for j in range(8):
                    ci = half * 8 + j
                    kb.op('pe', lambda e, j=j, ci=ci: e.transpose(btb[:, j * 128:(j + 1) * 128], xbcT[:, ci, q0:q0 + 128], ident),
                          reads=[('xbc', ci), 'cb'], writes=[btkey], inc=(j == 7))
                src = btb.rearrange("p (i e d) -> p i e d", i=8, e=2)
                hs = half * 16
                dtE = dtt[:, hs:hs + 16:2].unsqueeze(2).to_broadcast([128, 8, 64])
                dtO = dtt[:, hs + 1:hs + 16:2].unsqueeze(2).to_broadcast([128, 8, 64])
                kb.op('dve', lambda e: e.tensor_tensor(out=dxE[:, half * 8:half * 8 + 8, 0:64], in0=src[:, :, 0, :], in1=dtE, op=ALU.mult),
                      reads=[btkey, kdtt], writes=['dxE'])
                kb.op('dve', lambda e: e.tensor_tensor(out=dxO[:, half * 8:half * 8 + 8, 64:128], in0=src[:, :, 1, :], in1=dtO, op=ALU.mult),
                      reads=[btkey, kdtt], writes=['dxO'])
                kb.op('dve', lambda e: e.tensor_tensor(
                    out=dxe[:, half * 1024:(half + 1) * 1024].rearrange("p (h d) -> p h d", h=16),
                    in0=btb.rearrange("p (h d) -> p h d", h=16),
                    in1=dte[:, hs:hs + 16].unsqueeze(2).to_broadcast([128, 16, 64]), op=ALU.mult),
                    reads=[btkey, kdte], writes=['dxe'])
                yield
            bt, btkey = bank()
            btb = bt[:].bitcast(BF16)
            for g in range(4):
                kb.op('pe', lambda e, g=g: e.transpose(btb[:, g * 128:(g + 1) * 128], xbcT[:, 16 + g, q0:q0 + 128], ident),
                      reads=[('xbc', 16 + g), 'cb'], writes=[btkey], inc=(g == 3))
            kb.op('act', lambda e: e.activation(out=Btok[:], in_=btb[:, 0:512], func=AF.Copy), reads=[btkey], writes=['Btok'])

            yield
            def prep(g):
                BT = xbcT[:, 16 + g, q0:q0 + 128]
                CT = xbcT[:, 20 + g, q0:q0 + 128]
                bk, bkey = bank()
                kb.op('pe', lambda e: e.matmul(bk[:, 0:128], lhsT=BT, rhs=CT, start=True, stop=True),
                      reads=[('xbc', 16 + g), ('xbc', 20 + g)], writes=[bkey])
                ei = rot('se')
                kb.op('dve', lambda e: e.tensor_tensor(out=cbm[ei][:], in0=bk[:, 0:128], in1=Um, op=ALU.mult), reads=[bkey, 'cb'], writes=[('cbm', ei)])
                Ub = Um.unsqueeze(1).to_broadcast([128, 8, 128])
                kb.op('dve', lambda e: e.tensor_tensor(out=AUh[ei][:], in0=Ub, in1=ahi[:, g * 8:g * 8 + 8].unsqueeze(2).to_broadcast([128, 8, 128]), op=ALU.mult),
                      reads=['cb', kahi], writes=[('AUh', ei)])
                kb.op('dve', lambda e: e.tensor_tensor(out=AUl[ei][:], in0=Ub, in1=alo[:, g * 8:g * 8 + 8].unsqueeze(2).to_broadcast([128, 8, 128]), op=ALU.mult),
                      reads=['cb', kalo], writes=[('AUl', ei)])
                return ei

            def seg(ei):
                for hh in range(2):
                    bs, bskey = bank()
                    kb.op('pe', lambda e: e.matmul(bs[:, 0:512], lhsT=Lm, rhs=AUh[ei][:, hh * 4:hh * 4 + 4, :], start=True, stop=False),
                          reads=['cb', ('AUh', ei)], writes=[bskey], inc=False)
                    kb.op('pe', lambda e: e.matmul(bs[:, 0:512], lhsT=Lm, rhs=AUl[ei][:, hh * 4:hh * 4 + 4, :], start=False, stop=True),
                          reads=['cb', ('AUl', ei)], writes=[bskey])
                    kb.op('act', lambda e: e.activation(out=Eb[ei][:, hh * 4:hh * 4 + 4, :], in_=bs[:, 0:512].rearrange("p (h l) -> p h l", h=4), func=AF.Exp),
                          reads=[bskey], writes=[('Eb', ei)])

            def ebcbm(ei):
                kb.op('dve', lambda e: e.tensor_tensor(out=Eb[ei][:], in0=Eb[ei][:], in1=cbm[ei][:].unsqueeze(1).to_broadcast([128, 8, 128]), op=ALU.mult),
                      reads=[('Eb', ei), ('cbm', ei)], writes=[('Eb', ei)])

            eis = [prep(0)]
            seg(eis[0])
            eis.append(prep(1))
            ebcbm(eis[0])
            yield
            for g in range(4):
                CT = xbcT[:, 20 + g, q0:q0 + 128]
                ei = eis[g]
                if g + 1 < 4:
                    seg(eis[g + 1])
                if g + 2 < 4:
                    eis.append(prep(g + 2))
                yield
                stage_lists = []
                for pr in range(4):
                    ci = g * 4 + pr
                    he, ho = 2 * pr, 2 * pr + 1
                    by, bykey = bank()
                    kb.op('pe', lambda e: e.matmul(by[:, 0:128], lhsT=dxE[:, ci, :], rhs=Eb[ei][:, he, :], start=True, stop=False),
                          reads=['dxE', ('Eb', ei)], writes=[bykey], inc=False)
                    kb.op('pe', lambda e: e.matmul(by[:, 0:128], lhsT=dxO[:, ci, :], rhs=Eb[ei][:, ho, :], start=False, stop=True),
                          reads=['dxO', ('Eb', ei)], writes=[bykey], inc=False)
                    kb.op('pe', lambda e: e.matmul(by[:, 256:384], lhsT=abcH[:, ci * 2:ci * 2 + 2, :], rhs=Um, start=True, stop=False),
                          reads=['abcH', 'cb'], writes=[bykey], inc=False)
                    kb.op('pe', lambda e: e.matmul(by[:, 256:384], lhsT=abcL[:, ci * 2:ci * 2 + 2, :], rhs=Um, start=False, stop=True),
                          reads=['abcL', 'cb'], writes=[bykey], inc=False)
                    if not sample:
                        kb.op('pe', lambda e: e.matmul(by[:, 128:256], lhsT=Sb[:, ci * 128:(ci + 1) * 128], rhs=CT, start=True, stop=True),
                              reads=[('Sb', g), ('xbc', 20 + g)], writes=[bykey])
                        yoff_ap, yoff_keys = by[:, 128:256], [bykey]
                    else:
                        kb.op('pe', lambda e: e.matmul(by[:, 128:136], lhsT=ones, rhs=ones[:, 0:8], start=True, stop=True), reads=['cb'], writes=[bykey])
                        yoff_ap, yoff_keys = S[:, ci * 128:(ci + 1) * 128], ['S']
                    yi = rot('y', 4)

                    def mk(ci=ci, by=by, bykey=bykey, yi=yi, yoff_ap=yoff_ap, yoff_keys=yoff_keys):
                        return [
                            lambda: kb.op('act', lambda e: e.activation(out=ea[yi][:], in_=by[:, 256:384], func=AF.Exp), reads=[bykey], writes=[('ea', yi)]),
                            lambda: kb.op('dve', lambda e: e.tensor_tensor(out=yt[yi][:], in0=yoff_ap, in1=ea[yi][:], op=ALU.mult),
                                          reads=yoff_keys + [('ea', yi)], writes=[('yt', yi)]),
                            lambda: kb.op('dve', lambda e: e.tensor_tensor(out=yt[yi][:], in0=by[:, 0:128], in1=yt[yi][:], op=ALU.add),
                                          reads=[bykey, ('yt', yi)], writes=[('yt', yi)]),
                            lambda: kb.op('dve', lambda e: e.scalar_tensor_tensor(out=yt[yi][:], in0=xbcT[:, ci, q0:q0 + 128], scalar=prm[:, P_D + ci:P_D + ci + 1],
                                                                                  in1=yt[yi][:], op0=ALU.mult, op1=ALU.add),
                                          reads=[('xbc', ci), 'prm', ('yt', yi)], writes=[('yt', yi)]),
                            lambda: kb.op('dve', lambda e: e.tensor_tensor(out=yb[:, ci, :], in0=yt[yi][:], in1=zsT[:, ci, q0:q0 + 128], op=ALU.mult),
                                          reads=[('yt', yi), ('zs', ci)], writes=[('yb', ci)]),
                            lambda: kb.op('act', lambda e: e.activation(out=ysq[:, ci, :], in_=yb[:, ci, :], func=AF.Square), reads=[('yb', ci)], writes=[('ysq', ci)]),
                        ]
                    stage_lists.append(mk())
                for k in range(6):
                    for l_ in stage_lists:
                        l_[k]()
                if g + 1 < 4:
                    ebcbm(eis[g + 1])
                yield
            if not sample:
                sbanks = []
                for g in range(4):
                    bs, bskey = bank()
                    kb.op('pe', lambda e: e.matmul(bs[:, 0:512], lhsT=Btok[:, g * 128:(g + 1) * 128], rhs=dxe[:, g * 512:(g + 1) * 512], start=True, stop=True),
                          reads=['Btok', 'dxe'], writes=[bskey])
                    sbanks.append((bs, bskey))
                for g in range(4):
                    Sg = S[:, g * 512:(g + 1) * 512]
                    kb.op('dve', lambda e: e.tensor_tensor(out=Sg.rearrange("p (h d) -> p h d", h=8), in0=Sg.rearrange("p (h d) -> p h d", h=8),
                                                           in1=cdb[:, g * 8:g * 8 + 8].unsqueeze(2).to_broadcast([128, 8, 64]), op=ALU.mult),
                          reads=[('S', g), kcdb], writes=[('S', g)])
                for g in range(4):
                    bs, bskey = sbanks[g]
                    Sg = S[:, g * 512:(g + 1) * 512]
                    kb.op('dve', lambda e: e.tensor_tensor(out=Sg, in0=Sg, in1=bs[:, 0:512], op=ALU.add), reads=[('S', g), bskey], writes=[('S', g)])
                for g in range(4):
                    Sg = S[:, g * 512:(g + 1) * 512]
                    kb.op('act', lambda e: e.activation(out=Sb[:, g * 512:(g + 1) * 512], in_=Sg, func=AF.Copy), reads=[('S', g)], writes=[('Sb', g)])
                yield
            if sample:
                ssd_sample_state()
            bn, bnkey = bank()
            for g in range(4):
                for pr in range(4):
                    ci = g * 4 + pr
                    kb.op('pe', lambda e: e.matmul(bn[:, g * 128:(g + 1) * 128], lhsT=ones, rhs=ysq[:, ci, :], start=(pr == 0), stop=(pr == 3)),
                          reads=['cb', ('ysq', ci)], writes=[bnkey], inc=(g == 3 and pr == 3))
            kb.op('act', lambda e: e.activation(out=rsn[:], in_=bn[:, 0:512].rearrange("p (g l) -> p g l", g=4), func=AF.Ln, bias=EPS, scale=1.0 / 512),
                  reads=[bnkey], writes=['rsn'])
            kb.op('act', lambda e: e.activation(out=rsn[:], in_=rsn[:], func=AF.Exp, scale=-0.5), reads=['rsn'], writes=['rsn'])
            for ci in range(16):
                kb.op('dve', lambda e, ci=ci: e.scalar_tensor_tensor(out=ynT[:, ci, q0:q0 + 128], in0=yb[:, ci, :], scalar=prm[:, P_NWS + ci:P_NWS + ci + 1],
                                                                     in1=rsn[:, ci // 4, :], op0=ALU.mult, op1=ALU.mult),
                      reads=[('yb', ci), 'prm', 'rsn'], writes=[('yn', c, ci)])
            yield

        def ssd_sample_pre():
            yoffs = S[:].rearrange("p (c l) -> p c l", c=16)
            h0bufs = [ynT[:, :, 128:256], zsT[:, :, 128:256]]
            for b in range(NB):
                j = b % 2
                kb.dma('pool', h0bufs[j], ssT_d[b].rearrange("n (c m) -> n c m", c=16), writes=[('h0T', j)])
                bk, bkey = bank()
                for ci in range(16):
                    kb.op('pe', lambda e: e.matmul(bk[:, ci * TS:(ci + 1) * TS], lhsT=h0bufs[j][:, ci, :], rhs=xbcT[:, 20 + ci // 4, b * TS:(b + 1) * TS],
                                                   start=True, stop=True), reads=[('h0T', j), ('xbc', 20 + ci // 4)], writes=[bkey], inc=(ci == 15))
                kb.op('act', lambda e: e.activation(out=yoffs[:, :, b * TS:(b + 1) * TS], in_=bk[:, 0:128].rearrange("p (c t) -> p c t", c=16), func=AF.Copy),
                      reads=[bkey], writes=['S'])

        def ssd_sample_state():
            bk, bkey = bank()
            for ci in range(16):
                kb.op('pe', lambda e: e.matmul(bk[:, ci * 16:(ci + 1) * 16], lhsT=abcH[:, ci * 2:ci * 2 + 2, :], rhs=cb[:, C_SEL:C_SEL + 16], start=True, stop=False),
                      reads=['abcH', 'cb'], writes=[bkey], inc=False)
                kb.op('pe', lambda e: e.matmul(bk[:, ci * 16:(ci + 1) * 16], lhsT=abcL[:, ci * 2:ci * 2 + 2, :], rhs=cb[:, C_SEL:C_SEL + 16], start=False, stop=True),
                      reads=['abcL', 'cb'], writes=[bkey], inc=(ci == 15))
            kb.op('act', lambda e: e.activation(out=cdall[:], in_=bk[:, 0:256].rearrange("p (c b) -> p c b", c=16), func=AF.Exp), reads=[bkey], writes=['cdall'])
            kb.op('pool', lambda e: e.memset(S[:, 0:1], 0.0), writes=['S', ('Su', 0), ('Su', 1)])
            kb.op('pool', lambda e: e.memset(Sb[:, 0:2], 0.0), writes=['Sb', ('Su', 2)])
            units = [
                (gT[:].rearrange("p a b -> p (a b)")[:, 0:2048].bitcast(F32), [('g', c_) for c_ in range(8)]),
                (gT[:].rearrange("p a b -> p (a b)")[:, 2048:4096].bitcast(F32), [('g', c_) for c_ in range(8, 16)]),
                (qT[:].rearrange("p a b -> p (a b)")[:, 0:2048].bitcast(F32), [('qT', c_) for c_ in range(8)]),
                (S[:, 0:1024], [('Su', 0)]),
                (S[:, 1024:2048], [('Su', 1)]),
                (Sb[:, 0:2048].bitcast(F32), [('Su', 2)]),
            ]
            def load_state(b):
                res = []
                for hh in range(2):
                    uap, ukeys = units[(2 * b + hh) % 6]
                    h0 = uap.rearrange("p (c n) -> p c n", c=8)
                    kb.dma('sp', h0, ss_d[b, hh * 1024:(hh + 1) * 1024, :].rearrange("(c p) n -> p c n", p=128), writes=ukeys)
                    res.append((h0, ukeys))
                return res

            loaded = {0: load_state(0), 1: load_state(1)}
            for b in range(NB):
                bi = rot('bm')
                kb.op('dve', lambda e: e.tensor_scalar(out=Bm[bi][:], in0=Btok[:], scalar1=cf[:, F_SEL + b:F_SEL + b + 1], scalar2=None, op0=ALU.mult),
                      reads=['Btok', 'cf'], writes=[('P', bi)])
                if b + 2 < NB:
                    loaded[b + 2] = load_state(b + 2)
                for hh in range(2):
                    h0, ukeys = loaded[b][hh]
                    for q4 in range(2):
                        c0 = hh * 8 + q4 * 4
                        bs_, bskey = bank()
                        for k4 in range(4):
                            ci = c0 + k4
                            kb.op('pe', lambda e: e.matmul(bs_[:, k4 * 128:(k4 + 1) * 128], lhsT=dxe[:, ci * 128:(ci + 1) * 128],
                                                           rhs=Bm[bi][:, (ci // 4) * 128:(ci // 4 + 1) * 128], start=True, stop=True),
                                  reads=['dxe', ('P', bi)], writes=[bskey], inc=(k4 == 3))
                        hv = h0[:, q4 * 4:q4 * 4 + 4, :]
                        kb.op('dve', lambda e: e.tensor_tensor(out=hv, in0=hv, in1=cdall[:, c0:c0 + 4, b:b + 1].to_broadcast([128, 4, 128]), op=ALU.mult),
                              reads=ukeys + ['cdall'], writes=ukeys)
                        kb.op('dve', lambda e: e.tensor_tensor(out=hv, in0=hv, in1=bs_[:, 0:512].rearrange("p (c n) -> p c n", c=4), op=ALU.add),
                              reads=ukeys + [bskey], writes=ukeys)
                    kb.dma('act', sss[b, hh * 1024:(hh + 1) * 1024, :].rearrange("(c p) n -> p c n", p=128), h0, reads=ukeys)

        dtall = sb("dtall", [128, NT // 128, 32], F32)
        sinkx = sb("sinkx", [128, 16, TS])
        cdall = sb("cdall", [128, 16, 16], F32)
        dxe3 = dxe[:].rearrange("p (a l) -> p a l", a=16)
        kcE = [Eb[0][:, 0:4, :], dxe3[:, 0:4, :]]
        kcO = [Eb[0][:, 4:8, :], dxe3[:, 4:8, :]]
        vcE = [Eb[1][:, 0:4, :], dxe3[:, 8:12, :]]
        vcO = [Eb[1][:, 4:8, :], dxe3[:, 12:16, :]]
        Bm = Pt
        rr['bm'] = 0
        kb.op('dve', lambda e: e.tensor_copy(out=sinkx[:], in_=esink[:].unsqueeze(2).to_broadcast([128, 16, TS])), reads=['esink'], writes=['sinkx'])
        hists = S[:, 0:24 * NB * 3].rearrange("p (c x) -> p c x", c=24)

        tiles = []
        for s_ in range(NSEQ):
            for t0 in range(0, SEQ, NT):
                r0 = s_ * SEQ + t0
                tiles.append((xp[r0:r0 + NT, :], NT, False, s_, t0, yp[r0:r0 + NT, :]))
        if max_tiles is not None:
            tiles = tiles[:max_tiles]
        if do_sample:
            tiles.append((xs, 128, True, 0, 0, ys))
        wstate['total'] = NG * len(tiles)
        gens = [run_tile(*a_) for a_ in tiles]

        def run_until(g_, tag):
            for v in g_:
                if v == tag:
                    return

        if gens:
            run_until(gens[0], 'front_done')
        for i, cur in enumerate(gens):
            run_until(cur, 'body_done')
            nxt = gens[i + 1] if i + 1 < len(gens) else None
            done_c, done_n = False, nxt is None
            while not (done_c and done_n):
                if not done_c:
                    try:
                        next(cur)
                    except StopIteration:
                        done_c = True
                if not done_n:
                    try:
                        if next(nxt) == 'front_done':
                            done_n = True
                    except StopIteration:
                        done_n = True
        kb.finish()
        print("instructions:", kb.ninst, "waits:", kb.nwait, "per-engine:", kb.cnt)
    return nc


def _consts():
    cb = np.zeros((128, NCB), np.float32)
    p = np.arange(128)
    cb[:, C_ID:C_ID + 128] = np.eye(128)
    cb[:, C_ONES:C_ONES + 128] = 1.0
    cb[:, C_ONESE:C_ONESE + 64] = 1.0
    cb[:, C_ONESO + 64:C_ONESO + 128] = 1.0
    cb[:, C_BLK:C_BLK + 128] = (p[:, None] // 64 == p[None, :] // 64)
    rt = np.zeros((128, 128), np.float32)
    for k in range(128):
        d = k % 64
        if d < 8:
            rt[k, k + 8] = 1.0
        elif d < 16:
            rt[k, k - 8] = -1.0
    cb[:, C_RT:C_RT + 128] = rt
    key, q = p[:, None], p[None, :]
    mp = (key > q).astype(np.float32)
    mc = (q >= key).astype(np.float32)
    cb[:, C_M4:C_M4 + 512] = np.concatenate([mp, mc, mp, mc], 1)
    cb[:, C_U:C_U + 128] = (key <= q)
    cb[:, C_L:C_L + 128] = (key > q)
    same = (key // TS == q // TS)
    cb[:, C_UB:C_UB + 128] = (key <= q) & same
    cb[:, C_LB:C_LB + 128] = (key > q) & same
    cb[:, C_SEL:C_SEL + 16] = (p[:, None] // TS == np.arange(16)[None, :])
    cb[:, C_MSC:C_MSC + 8] = (p[:, None] > np.arange(8)[None, :])
    bq = np.arange(128)[None, :]
    cb[:, C_MSN:C_MSN + 128] = ((p[:, None] // TS) == (bq // TS)) & ((p[:, None] % TS) <= (bq % TS))
    cb[0, C_E0:C_E0 + 128] = 1.0
    cf = np.zeros((128, NCF), np.float32)
    cf[:, F_ID:F_ID + 128] = np.eye(128)
    half = 8
    inv_freq = np.power(np.float32(500000.0), -np.arange(half, dtype=np.float32) * np.float32(2.0 / 16)).astype(np.float32)

    def tables(pos):
        invf64 = np.power(500000.0, -np.arange(half, dtype=np.float64) * (2.0 / 16))
        ang = pos.astype(np.float64)[None, :] * invf64[:, None]
        cos8, sin8 = np.cos(ang).astype(np.float32), np.sin(ang).astype(np.float32)
        ct = np.ones((128, pos.shape[0]), np.float32)
        st = np.zeros((128, pos.shape[0]), np.float32)
        for pp in range(128):
            d = pp % 64
            if d < 16:
                ct[pp] = cos8[d % 8]
                st[pp] = sin8[d % 8]
        return ct, st
    cp, sp_ = tables(np.arange(SEQ))
    rope = np.stack([cp, sp_], 0)
    cs, ss_ = tables(PAST + (np.arange(128) % TS))
    cf[:, F_COSS:F_COSS + 128] = cs
    cf[:, F_SEL:F_SEL + 16] = (np.arange(128)[:, None] // TS == np.arange(16)[None, :])
    cf[:, F_SINS:F_SINS + 128] = ss_
    return cb, cf, rope


def _grp(w, rows_kc, col0):
    blk_ = w[:, col0:col0 + 512]
    return np.ascontiguousarray(blk_.reshape(rows_kc, 128, 512).transpose(1, 0, 2))


def _layout_weights(w_in, w_attn, w_ssd, w_out):
    cols = list(range(OFF_Q, OFF_Q + 1024))
    for g in range(4):
        kc = list(range(OFF_K + g * 64, OFF_K + (g + 1) * 64))
        cols += kc + kc
    cols += list(range(OFF_ZA, OFF_ZA + 1024))
    cols += list(range(OFF_ZS, OFF_ZS + 2048))
    cols += list(range(OFF_XBC, OFF_XBC + 3072))
    cols += list(range(OFF_G, OFF_G + 2048))
    wp = w_in[:, cols]
    groups = [_grp(wp, 8, g * 512) for g in range(19)]
    groups += [_grp(w_attn, 8, g * 512) for g in range(2)]
    for g in range(2):
        groups += [_grp(w_ssd[0:1024], 8, g * 512), _grp(w_ssd[1024:2048], 8, g * 512)]
    groups += [_grp(w_out, 8, g * 512) for g in range(2)]
    wall = np.stack(groups, 0)
    colsA = list(range(OFF_V, OFF_V + 256)) + list(range(OFF_DT, OFF_DT + 32))
    wA = np.ascontiguousarray(w_in[:, colsA].reshape(8, 128, 288).transpose(1, 0, 2))
    return wall, wA


def _params(norm_w, q_norm_w, k_norm_w, sinks, conv_w, conv_b, dt_bias, A_log, D_skip, ssd_norm_w):
    prm = np.zeros((128, NPRM), np.float32)
    p = np.arange(128)
    prm[:, P_NW:P_NW + 8] = norm_w.reshape(8, 128).T
    prm[:, P_WQ] = q_norm_w[p % 64]
    prm[:, P_WK] = k_norm_w[p % 64]
    prm[:, P_CW:P_CW + 96] = conv_w.reshape(4, 24, 128).transpose(2, 1, 0).reshape(128, 96)
    prm[:, P_CB:P_CB + 24] = conv_b.reshape(24, 128).T
    prm[:, P_DTB:P_DTB + 32] = dt_bias[None, :]
    prm[:, P_ALOG:P_ALOG + 32] = A_log[None, :]
    prm[:, P_D:P_D + 16] = D_skip[(np.arange(16)[None, :] * 128 + p[:, None]) // 64]
    prm[:, P_NWS:P_NWS + 16] = ssd_norm_w.reshape(16, 128).T
    prm[:, P_SINK:P_SINK + 16] = sinks[None, :]
    return prm


_CACHE = {}


def kernel(x_prompt, x_sample, cache_k, cache_v, state_conv, state_ssm,
           norm_w, w_in, q_norm_w, k_norm_w, sinks, conv_w, conv_b, dt_bias,
           A_log, D_skip, ssd_norm_w, w_attn_proj, w_ssd_proj, w_out, _do_sample=True):
    f = lambda a: np.ascontiguousarray(np.asarray(a, dtype=np.float32))
    x_prompt, x_sample, cache_k, cache_v, state_conv, state_ssm = map(f, (x_prompt, x_sample, cache_k, cache_v, state_conv, state_ssm))
    wall, wA = _layout_weights(f(w_in)[0], f(w_attn_proj)[0], f(w_ssd_proj)[0], f(w_out)[0])
    prm = _params(f(norm_w)[0], f(q_norm_w)[0], f(k_norm_w)[0], f(sinks)[0], f(conv_w)[0], f(conv_b)[0],
                  f(dt_bias)[0], f(A_log)[0], f(D_skip)[0], f(ssd_norm_w)[0])
    cb, cf, rope = _consts()
    if 'nc' not in _CACHE:
        _CACHE['nc'] = build_program(_do_sample)
    nc = _CACHE['nc']
    in_maps = []
    for c in range(NCORES):
        bs = slice(c * NB, (c + 1) * NB)
        in_maps.append({
            "xp": x_prompt[c * NSEQ:(c + 1) * NSEQ].reshape(NSEQ * SEQ, D),
            "xs": x_sample[bs].reshape(NB * TS, D),
            "wall": wall, "wA": wA, "prm": prm, "cb": cb, "cf": cf, "rope": rope,
            "ck": cache_k[0, bs].reshape(NB, 128, 256),
            "cv": cache_v[0, bs].reshape(NB, 128, 256),
            "ckT": np.ascontiguousarray(cache_k[0, bs].reshape(NB, 128, 256).transpose(0, 2, 1)),
            "sconvT": np.ascontiguousarray(state_conv[0, bs].reshape(NB * 3, 3072).T),
            "sssm": state_ssm[0, bs].reshape(NB, 2048, 128),
            "sssmT": np.ascontiguousarray(state_ssm[0, bs].reshape(NB, 2048, 128).transpose(0, 2, 1)),
        })
    res = run_bass_kernel_spmd(nc, in_maps, core_ids=list(range(NCORES)))
    R = res.results
    cat = lambda k: np.concatenate([np.asarray(r[k], dtype=np.float32) for r in R], 0)
    y_p = cat("yp").reshape(16, SEQ, D)
    y_s = cat("ys").reshape(128, TS, D)
    kwp = cat("kwp").reshape(1, 16, 128, 4, 64)
    vwp = cat("vwp").reshape(1, 16, 128, 4, 64)
    cvp = cat("cvp").reshape(1, 16, 3, 3072)
    ssp = cat("ssp").reshape(1, 16, 32, 64, 128)
    kws = cat("kws").reshape(1, 128, 128, 4, 64)
    vws = cat("vws").reshape(1, 128, 128, 4, 64)
    cvs = cat("cvs").reshape(1, 128, 3, 3072)
    sss = cat("sss").reshape(1, 128, 32, 64, 128)
    return (y_p, y_s, kwp, vwp, cvp, ssp, kws, vws, cvs, sss)
```
